# Optimizing a Trainium2 kernel written in Bass

```python
import math
import jax, jax.numpy as jnp
from jax import lax
import numpy as np

D_MODEL = 1024
BATCH = 16
SEQ = 2048
DEPTH = 1

D_MIX = D_MODEL
HEAD_DIM = 64
D_ATTN = D_MIX // 2
N_Q_HEADS = D_ATTN // HEAD_DIM
N_KV_HEADS = 2
Q_PER_KV = N_Q_HEADS // N_KV_HEADS
WINDOW = 128
BLOCK = WINDOW
D_SSM = D_MIX - D_ATTN
SSM_GROUP = 16
N_SSM_GROUPS = D_SSM // SSM_GROUP
STATE = 64
D_KV = N_KV_HEADS * HEAD_DIM
D_IN = D_ATTN + 2 * D_KV + D_SSM
D_FF = ((8 * D_MODEL // 3 + 127) // 128) * 128
N_MOD = 9
EPS = 1e-6

kernel_name = "hybrid_swa_sinks_s5_macaron_sandwich_adaln"


def rms_norm(x, g):
    xf = x.astype(jnp.float32)
    y = xf * lax.rsqrt(jnp.mean(xf * xf, axis=-1, keepdims=True) + EPS)
    return (y * g.astype(jnp.float32)).astype(x.dtype)


def modulate(h, shift, scale):
    return h * (1.0 + scale[:, None, :]) + shift[:, None, :]


def swiglu(h, w_gate, w_up, w_down):
    return (jax.nn.silu(h @ w_gate) * (h @ w_up)) @ w_down


def alibi_slopes():
    i = jnp.arange(1, N_Q_HEADS + 1, dtype=jnp.float32)
    return jnp.exp2(-8.0 * i / N_Q_HEADS)


def sliding_window_attention(q, k, v, sinks):
    b, s = q.shape[0], q.shape[1]
    nb = s // BLOCK
    qb = q.reshape(b, nb, BLOCK, N_KV_HEADS, Q_PER_KV, HEAD_DIM)
    kb = k.reshape(b, nb, BLOCK, N_KV_HEADS, HEAD_DIM)
    vb = v.reshape(b, nb, BLOCK, N_KV_HEADS, HEAD_DIM)
    pad = ((0, 0), (1, 0), (0, 0), (0, 0), (0, 0))
    kk = jnp.concatenate([jnp.pad(kb[:, :-1], pad), kb], axis=2)
    vv = jnp.concatenate([jnp.pad(vb[:, :-1], pad), vb], axis=2)

    scores = jnp.einsum('bnqkgd,bnjkd->bnkgqj', qb, kk).astype(jnp.float32) * (HEAD_DIM ** -0.5)

    qi = jnp.arange(BLOCK)[:, None]
    kj = jnp.arange(2 * BLOCK)[None, :]
    dist = qi + BLOCK - kj
    key_pos = jnp.arange(nb)[:, None, None] * BLOCK + kj[None] - BLOCK
    valid = (dist[None] >= 0) & (dist[None] < WINDOW) & (key_pos >= 0)

    slopes = alibi_slopes().reshape(N_KV_HEADS, Q_PER_KV)
    bias = -slopes[:, :, None, None] * dist.astype(jnp.float32)[None, None]
    scores = jnp.where(valid[None, :, None, None], scores + bias, -jnp.inf)

    sink = sinks.astype(jnp.float32).reshape(N_KV_HEADS, Q_PER_KV)[None, None, :, :, None, None]
    m = jnp.maximum(jnp.max(scores, axis=-1, keepdims=True), sink)
    p = jnp.exp(scores - m)
    p = p / (jnp.sum(p, axis=-1, keepdims=True) + jnp.exp(sink - m))
    out = jnp.einsum('bnkgqj,bnjkd->bnqkgd', p.astype(v.dtype), vv)
    return out.reshape(b, s, N_Q_HEADS * HEAD_DIM)


def s5_ssm(u, a_re, a_im, log_step, b_re, b_im, c_re, c_im, d_skip, w_glu, b_glu):
    f32 = jnp.float32
    bsz, s = u.shape[0], u.shape[1]
    uf = u.astype(f32).reshape(bsz, s, N_SSM_GROUPS, SSM_GROUP)
    lam = lax.complex(a_re.astype(f32), a_im.astype(f32))
    dt = jnp.exp(log_step.astype(f32))[:, None]
    lam_bar = jnp.exp(lam * dt)
    b_mat = lax.complex(b_re.astype(f32), b_im.astype(f32))
    b_bar = ((lam_bar - 1.0) / lam)[:, :, None] * b_mat
    bu = jnp.einsum('bsgh,gph->bsgp', uf.astype(jnp.complex64), b_bar)
    a = jnp.broadcast_to(lam_bar, bu.shape)

    def combine(left, right):
        a_l, b_l = left
        a_r, b_r = right
        return a_r * a_l, a_r * b_l + b_r

    _, states = lax.associative_scan(combine, (a, bu), axis=1)
    c_mat = lax.complex(c_re.astype(f32), c_im.astype(f32))
    y = jnp.real(jnp.einsum('bsgp,ghp->bsgh', states, c_mat))
    y = y + uf * d_skip.astype(f32).reshape(N_SSM_GROUPS, SSM_GROUP)
    y = jax.nn.gelu(y.reshape(bsz, s, D_SSM))
    y = y * jax.nn.sigmoid(y @ w_glu.astype(f32) + b_glu.astype(f32))
    return y.astype(u.dtype)


def setup_inputs(seed: int = 0) -> dict:
    key = jax.random.key(seed)
    ks = jax.random.split(key, 40)
    f32 = jnp.float32
    L = DEPTH
    nrm = lambda k, shape, s: jax.random.normal(k, shape, f32) * s
    gain = lambda k: 1.0 + 0.05 * jax.random.normal(k, (L, D_MODEL), f32)
    n_idx = jnp.arange(STATE, dtype=f32)
    return {
        "x": jax.random.normal(ks[0], (BATCH, SEQ, D_MODEL), f32),
        "c": jax.random.normal(ks[1], (BATCH, D_MODEL), f32),
        "w_ada": nrm(ks[2], (L, D_MODEL, N_MOD * D_MODEL), 0.5 * D_MODEL ** -0.5),
        "b_ada": nrm(ks[3], (L, N_MOD * D_MODEL), 0.02),
        "g_pre_ff1": gain(ks[4]),
        "g_post_ff1": gain(ks[5]),
        "w1_gate": nrm(ks[6], (L, D_MODEL, D_FF), D_MODEL ** -0.5),
        "w1_up": nrm(ks[7], (L, D_MODEL, D_FF), D_MODEL ** -0.5),
        "w1_down": nrm(ks[8], (L, D_FF, D_MODEL), D_FF ** -0.5),
        "g_pre_mix": gain(ks[9]),
        "g_post_mix": gain(ks[10]),
        "w_in": nrm(ks[11], (L, D_MODEL, D_IN), D_MODEL ** -0.5),
        "attn_sinks": nrm(ks[12], (L, N_Q_HEADS), 1.0),
        "ssm_a_re": -0.5 + 0.01 * jax.random.normal(ks[13], (L, N_SSM_GROUPS, STATE), f32),
        "ssm_a_im": math.pi * n_idx + 0.01 * jax.random.normal(ks[14], (L, N_SSM_GROUPS, STATE), f32),
        "ssm_log_step": jax.random.uniform(ks[15], (L, N_SSM_GROUPS), f32, math.log(1e-3), math.log(1e-1)),
        "ssm_b_re": nrm(ks[16], (L, N_SSM_GROUPS, STATE, SSM_GROUP), (2 * SSM_GROUP) ** -0.5),
        "ssm_b_im": nrm(ks[17], (L, N_SSM_GROUPS, STATE, SSM_GROUP), (2 * SSM_GROUP) ** -0.5),
        "ssm_c_re": nrm(ks[18], (L, N_SSM_GROUPS, SSM_GROUP, STATE), (2 * STATE) ** -0.5),
        "ssm_c_im": nrm(ks[19], (L, N_SSM_GROUPS, SSM_GROUP, STATE), (2 * STATE) ** -0.5),
        "ssm_d": nrm(ks[20], (L, D_SSM), 1.0),
        "ssm_w_glu": nrm(ks[21], (L, D_SSM, D_SSM), D_SSM ** -0.5),
        "ssm_b_glu": nrm(ks[22], (L, D_SSM), 0.02),
        "g_attn_out": 1.0 + 0.05 * jax.random.normal(ks[23], (L, D_ATTN), f32),
        "g_ssm_out": 1.0 + 0.05 * jax.random.normal(ks[24], (L, D_SSM), f32),
        "w_out": nrm(ks[25], (L, D_MIX, D_MODEL), D_MIX ** -0.5),
        "g_pre_ff2": gain(ks[26]),
        "g_post_ff2": gain(ks[27]),
        "w2_gate": nrm(ks[28], (L, D_MODEL, D_FF), D_MODEL ** -0.5),
        "w2_up": nrm(ks[29], (L, D_MODEL, D_FF), D_MODEL ** -0.5),
        "w2_down": nrm(ks[30], (L, D_FF, D_MODEL), D_FF ** -0.5),
    }


def reference(x, c, w_ada, b_ada, g_pre_ff1, g_post_ff1, w1_gate, w1_up, w1_down,
              g_pre_mix, g_post_mix, w_in, attn_sinks, ssm_a_re, ssm_a_im, ssm_log_step,
              ssm_b_re, ssm_b_im, ssm_c_re, ssm_c_im, ssm_d, ssm_w_glu, ssm_b_glu,
              g_attn_out, g_ssm_out, w_out, g_pre_ff2, g_post_ff2, w2_gate, w2_up, w2_down):
    bsz, s = x.shape[0], x.shape[1]
    c_act = jax.nn.silu(c)
    for l in range(DEPTH):
        mod = c_act @ w_ada[l] + b_ada[l]
        (sh1, sc1, ga1, sh2, sc2, ga2, sh3, sc3, ga3) = jnp.split(mod, N_MOD, axis=-1)

        h = modulate(rms_norm(x, g_pre_ff1[l]), sh1, sc1)
        f = swiglu(h, w1_gate[l], w1_up[l], w1_down[l])
        x = x + 0.5 * ga1[:, None, :] * rms_norm(f, g_post_ff1[l])

        h = modulate(rms_norm(x, g_pre_mix[l]), sh2, sc2)
        proj = h @ w_in[l]
        q = proj[..., :D_ATTN].reshape(bsz, s, N_Q_HEADS, HEAD_DIM)
        k = proj[..., D_ATTN:D_ATTN + D_KV].reshape(bsz, s, N_KV_HEADS, HEAD_DIM)
        v = proj[..., D_ATTN + D_KV:D_ATTN + 2 * D_KV].reshape(bsz, s, N_KV_HEADS, HEAD_DIM)
        u = proj[..., D_ATTN + 2 * D_KV:]
        attn = sliding_window_attention(q, k, v, attn_sinks[l])
        ssm = s5_ssm(u, ssm_a_re[l], ssm_a_im[l], ssm_log_step[l], ssm_b_re[l], ssm_b_im[l],
                     ssm_c_re[l], ssm_c_im[l], ssm_d[l], ssm_w_glu[l], ssm_b_glu[l])
        mixed = jnp.concatenate([rms_norm(attn, g_attn_out[l]), rms_norm(ssm, g_ssm_out[l])], axis=-1)
        mixed = mixed @ w_out[l]
        x = x + ga2[:, None, :] * rms_norm(mixed, g_post_mix[l])

        h = modulate(rms_norm(x, g_pre_ff2[l]), sh3, sc3)
        f = swiglu(h, w2_gate[l], w2_up[l], w2_down[l])
        x = x + 0.5 * ga3[:, None, :] * rms_norm(f, g_post_ff2[l])
    return x
```

```python
import contextlib
import math
import numpy as np
import concourse.bass as bass
import concourse.mybir as mybir
from concourse.bass_utils import run_bass_kernel_spmd

F32 = mybir.dt.float32
BF16 = mybir.dt.bfloat16
ALU = mybir.AluOpType
AF = mybir.ActivationFunctionType
AX = mybir.AxisListType

D = 1024
DFF = 2816
NCH = 22
TOK = 4096
EPS = 1e-6


class _Op:
    __slots__ = ("eng", "fn", "deps", "dma", "signal", "token", "prev_tok", "idx")


class Prog:
    ENGS = ("pe", "act", "dve", "pool", "sp")
    NDMA = {"sp": 4, "pool": 4, "act": 2}

    def __init__(self, nc, same_engine_sync=True):
        self.nc = nc
        self.ops = []
        self.last_w = {}
        self.readers = {}
        self.same_engine_sync = same_engine_sync
        self.bar = set()
        self.bar_start = 0

    def op(self, eng, fn, reads=(), writes=(), dma=False):
        o = _Op()
        o.eng, o.fn, o.dma = eng, fn, dma
        o.idx = len(self.ops)
        deps = set()
        for r in reads:
            w = self.last_w.get(r)
            if w is not None:
                deps.add(w)
        for r in writes:
            w = self.last_w.get(r)
            if w is not None:
                deps.add(w)
            for rd in self.readers.get(r, ()):
                deps.add(rd)
        for r in reads:
            self.readers.setdefault(r, []).append(o.idx)
        for r in writes:
            self.last_w[r] = o.idx
            self.readers[r] = []
        deps.discard(o.idx)
        fd = set(self.bar)
        for d in deps:
            do = self.ops[d]
            if do.eng == eng and not do.dma:
                if eng == "pe" or eng == "sp" or not self.same_engine_sync:
                    continue
            fd.add(d)
        o.deps = fd
        o.signal = dma
        o.token = None
        o.prev_tok = None
        self.ops.append(o)
        return o.idx

    def barrier(self):
        last = {}
        nb = set()
        for o in self.ops[self.bar_start:]:
            if o.dma:
                nb.add(o.idx)
            elif o.fn is not None:
                last[o.eng] = o.idx
        nb.update(last.values())
        self.bar = set(self.bar) | nb
        self.bar_start = len(self.ops)

    def emit(self):
        nc = self.nc
        ops = self.ops
        for o in ops:
            for d in o.deps:
                ops[d].signal = True
        with contextlib.ExitStack() as es:
            esem = {e: es.enter_context(nc.semaphore("s_" + e)) for e in self.ENGS}
            dsem = {
                e: [es.enter_context(nc.semaphore("d_%s%d" % (e, i))) for i in range(self.NDMA[e])]
                for e in ("sp", "pool", "act")
            }
            ecnt = {e: 0 for e in self.ENGS}
            dcnt = {e: 0 for e in dsem}
            for o in ops:
                if o.dma:
                    i = dcnt[o.eng]
                    dcnt[o.eng] += 1
                    nd = self.NDMA[o.eng]
                    s = dsem[o.eng][i % nd]
                    o.token = (s, 16 * (i // nd + 1))
                    if i >= nd:
                        o.prev_tok = (s, 16 * (i // nd))
                elif o.signal:
                    ecnt[o.eng] += 1
                    o.token = (esem[o.eng], ecnt[o.eng])
            assert max(ecnt.values()) < 60000, ecnt
            self.stats = dict(ecnt=ecnt, dcnt=dcnt, nops=len(ops))
            block = es.enter_context(nc.Block())

            def run(engname, e):
                known = {}
                for o in ops:
                    if o.eng != engname:
                        continue
                    toks = [ops[d].token for d in sorted(o.deps)]
                    if o.prev_tok is not None:
                        toks.append(o.prev_tok)
                    need = {}
                    for s, v in toks:
                        k = id(s)
                        if known.get(k, 0) >= v:
                            continue
                        if k not in need or need[k][1] < v:
                            need[k] = (s, v)
                    for k, (s, v) in need.items():
                        e.wait_ge(s, v)
                        known[k] = v
                    if o.fn is None:
                        continue
                    inst = o.fn(e)
                    if o.token is not None:
                        inst.then_inc(o.token[0], 16 if o.dma else 1)

            @block.tensor
            def _(e):
                run("pe", e)

            @block.scalar
            def _(e):
                run("act", e)

            @block.vector
            def _(e):
                run("dve", e)

            @block.gpsimd
            def _(e):
                run("pool", e)

            @block.sync
            def _(e):
                run("sp", e)


class Arena:
    def __init__(self, t, nwords):
        self.t = t
        self.n = nwords
        self.off = 0
        self.marks = []

    def f32(self, *shape):
        n = int(np.prod(shape))
        assert self.off + n <= self.n, ("arena overflow", self.off, n, self.n)
        ap = self.t[:, self.off:self.off + n]
        self.off += n
        return self._shape(ap, shape)

    def bf16(self, *shape):
        n = int(np.prod(shape))
        w = (n + 1) // 2
        assert self.off + w <= self.n, ("arena overflow", self.off, w, self.n)
        ap = self.t[:, self.off:self.off + w].bitcast(BF16)
        if 2 * w != n:
            ap = ap[:, 0:n]
        self.off += w
        return self._shape(ap, shape)

    @staticmethod
    def _shape(ap, shape):
        if len(shape) == 1:
            return ap
        if len(shape) == 2:
            return ap.rearrange("p (a b) -> p a b", a=shape[0])
        if len(shape) == 3:
            return ap.rearrange("p (a b c) -> p a b c", a=shape[0], b=shape[1])
        raise ValueError(shape)

    def push(self):
        self.marks.append(self.off)

    def pop(self):
        self.off = self.marks.pop()


def build(stage=3, nunits=4):
    nc = bass.Bass("TRN2", target_bir_lowering=False)

    def din(name, shape):
        return nc.dram_tensor(name, list(shape), F32, kind="ExternalInput").ap()

    x = din("x", [TOK, D])
    c = din("c", [2, D])
    w_ada = din("w_ada", [D, 9 * D])
    b_ada = din("b_ada", [9 * D])
    gvec = {n: din(n, [D]) for n in ("g_pre_ff1", "g_post_ff1", "g_pre_mix", "g_post_mix", "g_pre_ff2", "g_post_ff2")}
    wff = {}
    for i in (1, 2):
        wff[i] = (din("w%d_gate" % i, [D, DFF]), din("w%d_up" % i, [D, DFF]), din("w%d_down" % i, [DFF, D]))
    w_in = din("w_in", [D, 1280])
    sinks = din("attn_sinks", [8])
    a_re = din("ssm_a_re", [32, 64])
    a_im = din("ssm_a_im", [32, 64])
    log_step = din("ssm_log_step", [32])
    b_re = din("ssm_b_re", [32, 64, 16])
    b_im = din("ssm_b_im", [32, 64, 16])
    c_re = din("ssm_c_re", [512, 64])
    c_im = din("ssm_c_im", [512, 64])
    ssm_d = din("ssm_d", [512])
    w_glu = din("ssm_w_glu", [512, 512])
    b_glu = din("ssm_b_glu", [512])
    g_attn = din("g_attn_out", [512])
    g_ssm = din("g_ssm_out", [512])
    w_out = din("w_out", [D, D])
    out = nc.dram_tensor("out", [TOK, D], F32, kind="ExternalOutput").ap()
    x1d = nc.dram_tensor("x1_scr", [TOK, D], F32).ap()
    x2d = nc.dram_tensor("x2_scr", [TOK, D], F32).ap()
    tabs = nc.dram_tensor("tabs_scr", [2, 9 * D], F32).ap()
    smat_d = nc.dram_tensor("smat_scr", [128, 32 * 3 * 128], BF16).ap()

    NW = 53000
    with contextlib.ExitStack() as es:
        arena_t = es.enter_context(nc.sbuf_tensor("arena", [128, NW], F32))
        pst = [es.enter_context(nc.psum_tensor("ps%d" % i, [128, 512], F32)) for i in range(8)]
        ps = [t[:, :] for t in pst]
        psb = [t[:, :].bitcast(BF16) for t in pst]
        A = Arena(arena_t, NW)
        P = Prog(nc)

        def OP(eng, fn, r=(), w=(), dma=False):
            return P.op(eng, fn, r, w, dma)

        def MM(o, l, rr, st, sp_, r, w):
            OP("pe", lambda e: e.matmul(o, l, rr, start=st, stop=sp_), r, w)

        def TR(o, i, idn, r, w):
            OP("pe", lambda e: e.transpose(o, i, idn), r, w)

        def ACT(o, i, func, r, w, bias=None, scale=None, accum=None):
            kw = {}
            if bias is not None:
                kw["bias"] = bias
            if scale is not None:
                kw["scale"] = scale
            if accum is not None:
                kw["accum_out"] = accum
            OP("act", lambda e: e.activation(o, i, func, **kw), r, w)

        def TT(eng, o, a, b, op, r, w):
            OP(eng, lambda e: e.tensor_tensor(o, a, b, op), r, w)

        def TS(eng, o, a, s1, s2, op0, op1, r, w):
            if s2 is None:
                OP(eng, lambda e: e.tensor_scalar(o, a, s1, None, op0), r, w)
            else:
                OP(eng, lambda e: e.tensor_scalar(o, a, s1, s2, op0, op1), r, w)

        def STT(eng, o, a, s, b, op0, op1, r, w):
            OP(eng, lambda e: e.scalar_tensor_tensor(o, a, s, b, op0, op1), r, w)

        def CP(eng, o, i, r, w):
            if eng == "act":
                OP("act", lambda e: e.activation(o, i, AF.Copy), r, w)
            else:
                OP(eng, lambda e: e.tensor_copy(o, i), r, w)

        def MS(eng, o, val, r, w):
            OP(eng, lambda e: e.memset(o, val), r, w)

        def DMA(q, o, i, r, w, slow=False):
            if slow:
                OP(q, lambda e: e.dma_start(out=o, in_=i, allow_slow_non_contiguous=True), r, w, dma=True)
            else:
                OP(q, lambda e: e.dma_start(out=o, in_=i), r, w, dma=True)

        ident_b = A.bf16(128)
        ident_f = A.f32(128)
        perm_f = A.f32(128)
        ones_b = A.bf16(2)
        cst = A.f32(8)
        MS("pool", ident_f, 0.0, [], ["ident_f"])
        OP("pool", lambda e: e.affine_select(out=ident_f, in_=ident_f, compare_op=ALU.not_equal, fill=1.0, base=0,
                                             pattern=[[-1, 128]], channel_multiplier=1), ["ident_f"], ["ident_f"])
        CP("dve", ident_b, ident_f, ["ident_f"], ["ident_b"])
        CP("dve", perm_f[:, 0:64], ident_f[:, 64:128], ["ident_f"], ["perm_a"])
        CP("dve", perm_f[:, 64:128], ident_f[:, 0:64], ["ident_f"], ["perm_b"])
        MS("dve", ones_b, 1.0, [], ["ones_b"])
        MS("dve", cst[:, 0:1], EPS, [], ["cst0"])
        MS("dve", cst[:, 1:2], math.pi / 2, [], ["cst1"])
        MS("dve", cst[0:64, 2:3], 1.0, [], ["cst2a"])
        MS("dve", cst[64:128, 2:3], 0.0, [], ["cst2b"])
        MS("dve", cst[0:64, 3:4], 0.0, [], ["cst3a"])
        MS("dve", cst[64:128, 3:4], 1.0, [], ["cst3b"])
        MS("dve", cst[0:64, 4:5], -1.0, [], ["cst4a"])
        MS("dve", cst[64:128, 4:5], 1.0, [], ["cst4b"])
        CST = ["cst0", "cst1", "cst2a", "cst2b", "cst3a", "cst3b", "cst4a", "cst4b"]
        eps_c = cst[:, 0:1]

        AA = A.f32(2, 32)
        BB = A.f32(2, 32)
        Xc = A.f32(2, 32)
        atab = A.f32(8, 256)
        sink_t = A.f32(8)
        gattn_t = A.f32(512)
        gssm_c = A.f32(4)
        bglu_c = A.f32(4)
        modt_flat = A.f32(3 * D)
        modt = modt_flat.rearrange("p (a b) -> p a b", a=3)
        stat = A.f32(64)

        A.push()
        cT = A.f32(8, 2)
        wb = [A.f32(8, 512) for _ in range(4)]
        modr = arena_t[0:2, A.off:A.off + 9 * D]
        A.off += 9 * D
        b2 = arena_t[0:2, A.off:A.off + 9 * D]
        A.off += 9 * D
        g6 = arena_t[0:2, A.off:A.off + 6 * D]
        A.off += 6 * D
        for bb in range(2):
            DMA("sp", cT[:, :, bb], c[bb].rearrange("(k p) -> p k", p=128), [], ["cT"], slow=True)
        DMA("sp", b2, b_ada.partition_broadcast(2), [], ["b2"])
        for i, n in enumerate(("g_pre_ff1", "g_post_ff1", "g_pre_mix", "g_post_mix", "g_pre_ff2", "g_post_ff2")):
            DMA("sp", g6[:, i * D:(i + 1) * D], gvec[n].partition_broadcast(2), [], ["g6_%d" % i])
        ACT(cT, cT, AF.Silu, ["cT"], ["cT"])
        w_ada_v = w_ada.rearrange("(k p) n -> p k n", p=128)
        for cb in range(18):
            b = cb % 2
            wq = cb % 4
            DMA("sp", wb[wq], w_ada_v[:, :, cb * 512:(cb + 1) * 512], [], ["wb%d" % wq])
            for k in range(8):
                MM(ps[b][0:2, :], cT[:, k, :], wb[wq][:, k, :], k == 0, k == 7, ["cT", "wb%d" % wq], ["ps%d" % b])
            TT("dve", modr[:, cb * 512:(cb + 1) * 512], ps[b][0:2, :], b2[:, cb * 512:(cb + 1) * 512], ALU.add,
               ["ps%d" % b, "b2"], ["modr%d" % cb])
        for i in range(3):
            sc = modr[:, (3 * i + 1) * D:(3 * i + 2) * D]
            ga = modr[:, (3 * i + 2) * D:(3 * i + 3) * D]
            STT("dve", sc, sc, 1.0, g6[:, (2 * i) * D:(2 * i + 1) * D], ALU.add, ALU.mult,
                ["modr%d" % cb_ for cb_ in range(18)] + ["g6_%d" % (2 * i)], ["modr"])
            STT("dve", ga, ga, 0.5 if i != 1 else 1.0, g6[:, (2 * i + 1) * D:(2 * i + 2) * D], ALU.mult, ALU.mult,
                ["modr", "g6_%d" % (2 * i + 1)], ["modr"])
        DMA("sp", tabs, modr, ["modr"] + ["modr%d" % cb_ for cb_ in range(18)], ["tabs"])
        A.pop()
        P.barrier()

        def load_modt(i, seq):
            src = tabs[seq, 3 * i * D:(3 * i + 3) * D].partition_broadcast(128)
            DMA("sp", modt_flat, src, ["tabs"], ["modt"])

        def norm_transpose(src, r0, hT, tt, xt, tmp, hb, tagp, srcdep=(), mt=None, junk=None):
            if mt is None:
                mt = (modt, "modt")
            mtab, mname = mt
            b = tt % 2
            c0 = 10 * b
            sq_out = junk if junk is not None else tmp
            sq_w = [] if junk is not None else ["tmp"]
            DMA("sp", xt[b], src[r0:r0 + 128, :], list(srcdep), ["xt%d" % b])
            MS("dve", stat[:, c0:c0 + 1], 0.0, [], ["st0_%d" % b])
            ACT(sq_out, xt[b], AF.Square, ["xt%d" % b, "st0_%d" % b], sq_w + ["st0_%d" % b], accum=stat[:, c0:c0 + 1])
            ACT(stat[:, c0 + 1:c0 + 2], stat[:, c0:c0 + 1], AF.Sqrt, ["st0_%d" % b, "cst0"], ["st1_%d" % b], bias=eps_c, scale=1.0 / D)
            OP("dve", lambda e: e.reciprocal(stat[:, c0 + 2:c0 + 3], stat[:, c0 + 1:c0 + 2]), ["st1_%d" % b], ["st2_%d" % b])
            STT("dve", tmp, xt[b], stat[:, c0 + 2:c0 + 3], mtab[:, 1, :], ALU.mult, ALU.mult, ["xt%d" % b, "st2_%d" % b, mname, "tmp"], ["tmp"])
            hi = b % len(hb)
            TT("pool", hb[hi], tmp, mtab[:, 0, :], ALU.add, ["tmp", mname], ["hb%d" % hi])
            pb = psb[4 + b].rearrange("p (k n) -> p k n", k=8)
            for k in range(8):
                TR(pb[:, k, :], hb[hi][:, k * 128:(k + 1) * 128], ident_b, ["hb%d" % hi, "ident_b"], ["ps%d" % (4 + b)])
            CP("act", hT[:, :, tt * 128:(tt + 1) * 128], pb, ["ps%d" % (4 + b)], [tagp])

        def residual_out(src, dst, r0, b, xr, tmp, pss, srcdep=(), mt=None, junk=None):
            if mt is None:
                mt = (modt, "modt")
            mtab, mname = mt
            c0 = 4 + 10 * b
            DMA("sp", xr[b], src[r0:r0 + 128, :], list(srcdep), ["xr%d" % b])
            MS("dve", stat[:, c0:c0 + 2], 0.0, [], ["st4_%d" % b])
            for hf in range(2):
                ACT(junk[:, hf * 512:(hf + 1) * 512], ps[pss[hf]], AF.Square, ["ps%d" % pss[hf], "st4_%d" % b], ["st4_%d" % b],
                    accum=stat[:, c0 + hf:c0 + hf + 1])
            TT("dve", stat[:, c0 + 2:c0 + 3], stat[:, c0:c0 + 1], stat[:, c0 + 1:c0 + 2], ALU.add, ["st4_%d" % b], ["st6_%d" % b])
            ACT(stat[:, c0 + 3:c0 + 4], stat[:, c0 + 2:c0 + 3], AF.Sqrt, ["st6_%d" % b, "cst0"], ["st7_%d" % b], bias=eps_c, scale=1.0 / D)
            OP("dve", lambda e: e.reciprocal(stat[:, c0 + 4:c0 + 5], stat[:, c0 + 3:c0 + 4]), ["st7_%d" % b], ["st8_%d" % b])
            for hf in range(2):
                STT("dve", tmp[:, hf * 512:(hf + 1) * 512], ps[pss[hf]], stat[:, c0 + 4:c0 + 5], mtab[:, 2, hf * 512:(hf + 1) * 512],
                    ALU.mult, ALU.mult, ["ps%d" % pss[hf], "st8_%d" % b, mname, "tmpC"], ["tmpC"])
            TT("pool", xr[b], tmp, xr[b], ALU.add, ["tmpC", "xr%d" % b], ["xr%d" % b])
            DMA("sp", dst[r0:r0 + 128, :], xr[b], ["xr%d" % b], ["dst"])

        def ffn_phase(i, src, dst, srcdep=()):
            wg_d, wu_d, wd_d = wff[1 if i == 0 else 2]
            A.push()
            hT2 = [A.bf16(8, 1024), A.bf16(8, 1024)]
            actT = A.bf16(NCH, 1024)
            wg = [A.bf16(8, 256), A.bf16(8, 256)]
            wu = [A.bf16(8, 256), A.bf16(8, 256)]
            wdn = A.bf16(NCH, 1024)
            xt = [A.f32(D), A.f32(D)]
            xr = [A.f32(D), A.f32(D)]
            hb = [A.bf16(D), A.bf16(D)]
            sg = [A.f32(512), A.f32(512)]
            tmpA = A.f32(D)
            tmpC = A.f32(D)
            junk = A.bf16(D)
            malt_flat = A.f32(3 * D)
            malt = malt_flat.rearrange("p (a b) -> p a b", a=3)
            mts = [(modt, "modt"), (malt, "malt")]
            wg_v = wg_d.rearrange("(k p) n -> p k n", p=128)
            wu_v = wu_d.rearrange("(k p) n -> p k n", p=128)
            wd_v = wd_d.rearrange("(c p) n -> p c n", p=128)
            load_modt(i, 0)
            DMA("sp", malt_flat, tabs[1, 3 * i * D:(3 * i + 3) * D].partition_broadcast(128), ["tabs"], ["malt"])
            for cq in range(2):
                DMA("pool", wdn[:, cq * 11:(cq + 1) * 11, :], wd_v[:, cq * 11:(cq + 1) * 11, :], [], ["wdn%d" % cq])

            def a_tile(u, tt):
                norm_transpose(src, u * 1024 + tt * 128, hT2[u % 2], tt, xt, tmpA, hb, "hT%d" % (u % 2), srcdep, mts[u // 2], junk)

            for tt in range(8):
                a_tile(0, tt)
            for u in range(nunits):
                hT = hT2[u % 2]
                hTn = "hT%d" % (u % 2)
                j = 0
                for cp_ in range(11):
                    b = cp_ % 2
                    DMA("pool", wg[b], wg_v[:, :, cp_ * 256:(cp_ + 1) * 256], [], ["wg%d" % b])
                    DMA("pool", wu[b], wu_v[:, :, cp_ * 256:(cp_ + 1) * 256], [], ["wu%d" % b])
                    for cl in range(2):
                        ch = cp_ * 2 + cl
                        for st in range(2):
                            pg, pu = (j % 2) * 2, (j % 2) * 2 + 1
                            for k in range(8):
                                MM(ps[pg], wg[b][:, k, cl * 128:(cl + 1) * 128], hT[:, k, st * 512:(st + 1) * 512], k == 0, k == 7,
                                   ["wg%d" % b, hTn], ["ps%d" % pg])
                            for k in range(8):
                                MM(ps[pu], wu[b][:, k, cl * 128:(cl + 1) * 128], hT[:, k, st * 512:(st + 1) * 512], k == 0, k == 7,
                                   ["wu%d" % b, hTn], ["ps%d" % pu])
                            ACT(sg[j % 2], ps[pg], AF.Silu, ["ps%d" % pg], ["sg%d" % (j % 2)])
                            TT("dve", actT[:, ch, st * 512:(st + 1) * 512], sg[j % 2], ps[pu], ALU.mult,
                               ["sg%d" % (j % 2), "ps%d" % pu], ["actT"])
                            j += 1
                for tt in range(8):
                    if u + 1 < nunits:
                        a_tile(u + 1, tt)
                    pss = [6, 7]
                    for hf in range(2):
                        for ch in range(NCH):
                            MM(ps[pss[hf]], actT[:, ch, tt * 128:(tt + 1) * 128], wdn[:, ch, hf * 512:(hf + 1) * 512],
                               ch == 0, ch == NCH - 1, ["actT", "wdn0", "wdn1"], ["ps%d" % pss[hf]])
                    residual_out(src, dst, u * 1024 + tt * 128, tt % 2, xr, tmpC, pss, srcdep, mts[u // 2], junk)
            A.pop()
            P.barrier()

        def setup_mixer():
            MS("dve", cst[0:64, 5:6], -1.0, [], ["cst5a"])
            MS("dve", cst[64:128, 5:6], 0.0, [], ["cst5b"])
            MS("dve", cst[0:64, 6:7], 0.0, [], ["cst6a"])
            MS("dve", cst[64:128, 6:7], -1.0, [], ["cst6b"])
            selLo, selHi, sgn, nselLo, nselHi = cst[:, 2:3], cst[:, 3:4], cst[:, 4:5], cst[:, 5:6], cst[:, 6:7]
            DMA("sp", sink_t, sinks.partition_broadcast(128), [], ["sink_t"])
            DMA("sp", gattn_t, g_attn.partition_broadcast(128), [], ["gattn_t"])
            DMA("sp", gssm_c, g_ssm.rearrange("(c p) -> p c", p=128), [], ["gssm_c"], slow=True)
            DMA("sp", bglu_c, b_glu.rearrange("(c p) -> p c", p=128), [], ["bglu_c"], slow=True)
            MS("dve", Xc, 0.0, [], ["Xc"])
            A.push()
            dist = A.f32(256)
            v1 = A.f32(256)
            v2 = A.f32(256)
            OP("pool", lambda e: e.iota(dist, [[-1, 256]], base=128, channel_multiplier=1,
                                        allow_small_or_imprecise_dtypes=True), [], ["dist"])
            TS("dve", v1, dist, 0.0, None, ALU.is_ge, None, ["dist"], ["v1"])
            TS("dve", v2, dist, 128.0, None, ALU.is_lt, None, ["dist"], ["v2"])
            TT("dve", v1, v1, v2, ALU.mult, ["v1", "v2"], ["v1"])
            TS("dve", v2, v1, 30000.0, -30000.0, ALU.mult, ALU.add, ["v1"], ["v2"])
            for h in range(8):
                STT("dve", atab[:, h, :], dist, -(2.0 ** -(h + 1)), v1, ALU.mult, ALU.mult, ["dist", "v1"], ["atab"])
                TT("dve", atab[:, h, :], atab[:, h, :], v2, ALU.add, ["atab", "v2"], ["atab"])
            are = A.f32(32)
            aim = A.f32(32)
            lsb = A.f32(32)
            Bre = A.f32(32, 16)
            Bim = A.f32(32, 16)
            Cre = A.f32(32, 16)
            Cim = A.f32(32, 16)
            dcol = A.f32(32)
            mask = A.f32(128)
            for hf in range(2):
                sl = slice(hf * 64, (hf + 1) * 64)
                DMA("sp", are[sl, :], a_re.rearrange("g p -> p g"), [], ["are%d" % hf], slow=True)
                DMA("sp", aim[sl, :], a_im.rearrange("g p -> p g"), [], ["aim%d" % hf], slow=True)
                DMA("sp", Bre[sl], b_re.rearrange("g p h -> p g h"), [], ["Bre%d" % hf], slow=True)
                DMA("sp", Bim[sl], b_im.rearrange("g p h -> p g h"), [], ["Bim%d" % hf], slow=True)
            DMA("sp", lsb, log_step.partition_broadcast(128), [], ["lsb"])
            for j in range(8):
                DMA("sp", dcol[j * 16:(j + 1) * 16, :], ssm_d.rearrange("(g h) -> h g", h=16), [], ["dcol%d" % j], slow=True)
            DCOL = ["dcol%d" % j for j in range(8)]
            for nm, src_c, dstC in (("re", c_re, Cre), ("im", c_im, Cim)):
                for t in range(4):
                    cn = A.f32(2, 64)
                    for dd in range(2):
                        DMA("sp", cn[:, dd, :], src_c[t * 128:(t + 1) * 128, :], [], ["cn%s%d_%d" % (nm, t, dd)])
                    pb_ = ps[t % 2][:, 0:128]
                    TR(pb_, cn.rearrange("p a b -> p (a b)"), ident_f, ["cn%s%d_0" % (nm, t), "cn%s%d_1" % (nm, t), "ident_f"],
                       ["ps%d" % (t % 2)])
                    CP("act", dstC[:, t * 8:(t + 1) * 8, :].rearrange("p g h -> p (g h)"), pb_, ["ps%d" % (t % 2)], ["C" + nm])
            MS("dve", mask, 1.0, [], ["mask"])
            OP("pool", lambda e: e.affine_select(out=mask.rearrange("p (a b) -> p a b", a=8), in_=mask.rearrange("p (a b) -> p a b", a=8),
                                                 compare_op=ALU.is_ge, fill=0.0, base=15, pattern=[[16, 8], [0, 16]],
                                                 channel_multiplier=-1), ["mask"], ["mask"])
            cnt = [0]

            def T32():
                cnt[0] += 1
                return A.f32(32), "t32_%d" % cnt[0]

            def mul(a, b):
                o, n = T32()
                TT("dve", o, a[0], b[0], ALU.mult, [a[1], b[1]], [n])
                return (o, n)

            def add(a, b, op=ALU.add):
                o, n = T32()
                TT("dve", o, a[0], b[0], op, [a[1], b[1]], [n])
                return (o, n)

            def cmul(a, b):
                rr = add(mul(a[0], b[0]), mul(a[1], b[1]), ALU.subtract)
                ii = add(mul(a[0], b[1]), mul(a[1], b[0]))
                return (rr, ii)

            AR = (are, "are"); AI = (aim, "aim")
            CP("dve", are, are, ["are0", "are1"], ["are"])
            CP("dve", aim, aim, ["aim0", "aim1"], ["aim"])
            dt_, n_dt = T32()
            ACT(dt_, lsb, AF.Exp, ["lsb"], [n_dt])
            DT = (dt_, n_dt)
            XR = mul(DT, AR)
            XI = mul(DT, AI)
            s0, n_s0 = T32(); c0, n_c0 = T32(); m0, n_m0 = T32()
            ACT(s0, XI[0], AF.Sin, [XI[1]], [n_s0], scale=1.0 / 16)
            ACT(c0, XI[0], AF.Sin, [XI[1], "cst1"], [n_c0], bias=cst[:, 1:2], scale=1.0 / 16)
            ACT(m0, XR[0], AF.Exp, [XR[1]], [n_m0], scale=1.0 / 16)
            z = (mul((m0, n_m0), (c0, n_c0)), mul((m0, n_m0), (s0, n_s0)))
            for _ in range(4):
                z = cmul(z, z)
            LAM = z
            one, n_one = T32(); zero, n_zero = T32()
            MS("dve", one, 1.0, [], [n_one]); MS("dve", zero, 0.0, [], [n_zero])
            ONE = ((one, n_one), (zero, n_zero))
            pw = [ONE, LAM]
            for e_ in range(2, 9):
                pw.append(cmul(pw[-1], LAM))
            den = add(mul(LAM[0], LAM[0]), mul(LAM[1], LAM[1]))
            rden, n_rden = T32()
            OP("dve", lambda e: e.reciprocal(rden, den[0]), [den[1]], [n_rden])
            RDEN = (rden, n_rden)
            nli, n_nli = T32()
            TS("dve", nli, LAM[1][0], -1.0, None, ALU.mult, None, [LAM[1][1]], [n_nli])
            INV = (mul(LAM[0], RDEN), mul((nli, n_nli), RDEN))
            npw = [ONE, INV]
            for e_ in range(2, 8):
                npw.append(cmul(npw[-1], INV))
            nr, n_nr = T32()
            TS("dve", nr, LAM[0][0], -1.0, None, ALU.add, None, [LAM[0][1]], [n_nr])
            NR = (nr, n_nr)
            d2 = add(mul(AR, AR), mul(AI, AI))
            rd2, n_rd2 = T32()
            OP("dve", lambda e: e.reciprocal(rd2, d2[0]), [d2[1]], [n_rd2])
            RD2 = (rd2, n_rd2)
            W = (mul(add(mul(NR, AR), mul(LAM[1], AI)), RD2), mul(add(mul(LAM[1], AR), mul(NR, AI), ALU.subtract), RD2))
            cP = [cmul(npw[j], W) for j in range(8)]
            cPe = [cmul(pw[7 - j], W) for j in range(8)]
            cQ = [pw[j] for j in range(8)]
            cF = [pw[j + 1] for j in range(8)]

            def coef(cl, s_re_a, s_im_a, s_re_b, s_im_b, tag):
                ta = A.f32(32, 8)
                tb = A.f32(32, 8)
                for j in range(8):
                    (re_, nre), (im_, nim) = cl[j]
                    TS("dve", ta[:, :, j], re_, s_re_a, None, ALU.mult, None, [nre] + CST, [tag + "a"])
                    STT("dve", ta[:, :, j], im_, s_im_a, ta[:, :, j], ALU.mult, ALU.add, [nim, tag + "a"] + CST, [tag + "a"])
                    TS("dve", tb[:, :, j], re_, s_re_b, None, ALU.mult, None, [nre] + CST, [tag + "b"])
                    STT("dve", tb[:, :, j], im_, s_im_b, tb[:, :, j], ALU.mult, ALU.add, [nim, tag + "b"] + CST, [tag + "b"])
                return ta, tb

            CST2 = CST + ["cst5a", "cst5b", "cst6a", "cst6b"]
            CST[:] = CST2
            aP, bP = coef(cP, selLo, selHi, selHi, nselLo, "cfP")
            aE, bE = coef(cPe, selLo, selHi, selHi, nselLo, "cfE")
            aQ, bQ = coef(cQ, selLo, nselHi, nselHi, nselLo, "cfQ")
            aF, bF = coef(cF, selLo, nselHi, nselHi, nselLo, "cfF")
            tmp4 = A.f32(32, 128)

            def stack(ta, tb, Xre, Xim, tag, xr_names, xi_names):
                o = A.f32(32, 128)
                o4 = o.rearrange("p g (j h) -> p g j h", j=8)
                t4 = tmp4.rearrange("p g (j h) -> p g j h", j=8)
                TT("dve", o4, ta.unsqueeze(3).broadcast_to([128, 32, 8, 16]), Xre.unsqueeze(2).broadcast_to([128, 32, 8, 16]),
                   ALU.mult, [tag + "a"] + xr_names, ["st_" + tag])
                TT("dve", t4, tb.unsqueeze(3).broadcast_to([128, 32, 8, 16]), Xim.unsqueeze(2).broadcast_to([128, 32, 8, 16]),
                   ALU.mult, [tag + "b"] + xi_names, ["tmp4"])
                TT("dve", o, o, tmp4, ALU.add, ["st_" + tag, "tmp4"], ["st_" + tag])
                return o

            Pst = stack(aP, bP, Bre, Bim, "cfP", ["Bre0", "Bre1"], ["Bim0", "Bim1"])
            Est = stack(aE, bE, Bre, Bim, "cfE", ["Bre0", "Bre1"], ["Bim0", "Bim1"])
            Qst = stack(aQ, bQ, Cre, Cim, "cfQ", ["Cre"], ["Cim"])
            Fst = stack(aF, bF, Cre, Cim, "cfF", ["Cre"], ["Cim"])
            smat_sb = A.bf16(32, 3, 128)
            tmpT = [A.f32(128), A.f32(128)]
            for g in range(32):
                b = g % 2
                MM(ps[b][:, 0:128], Pst[:, g, :], Qst[:, g, :], True, True, ["st_cfP", "st_cfQ"], ["ps%d" % b])
                TT("dve", tmpT[b], ps[b][:, 0:128], mask, ALU.mult, ["ps%d" % b, "mask"], ["tmpT%d" % b])
                STT("dve", smat_sb[:, g, 0, :], ident_f, dcol[:, g:g + 1], tmpT[b], ALU.mult, ALU.add,
                    ["ident_f", "tmpT%d" % b] + DCOL, ["smat_sb"])
                TR(ps[2 + b][:, 0:128], Est[:, g, :], ident_f, ["st_cfE", "ident_f"], ["ps%d" % (2 + b)])
                CP("act", smat_sb[:, g, 1, :], ps[2 + b][:, 0:128], ["ps%d" % (2 + b)], ["smat_sb"])
            CP("pool", smat_sb[:, :, 2, :], Fst, ["st_cfF"], ["smat_sb"])
            DMA("sp", smat_d, smat_sb.rearrange("p g k c -> p (g k c)"), ["smat_sb"], ["smat_d"])
            CP("dve", AA[:, 0, :], pw[8][0][0], [pw[8][0][1]], ["AA"])
            CP("dve", AA[:, 1, :], pw[8][0][0], [pw[8][0][1]], ["AA"])
            TS("dve", BB[:, 0, :], pw[8][1][0], sgn, None, ALU.mult, None, [pw[8][1][1]] + CST, ["BB"])
            TS("dve", BB[:, 1, :], BB[:, 0, :], -1.0, None, ALU.mult, None, ["BB"], ["BB"])
            A.pop()
            P.barrier()

        if stage >= 2:
            setup_mixer()

        def mixer_phase(src, dst, srcdep):
            A.push()
            Wst = A.f32(2, 32, 128)
            h2T = Wst[:, 1].rearrange("p g n -> p (g n)").bitcast(BF16).rearrange("p (k n) -> p k n", k=8)
            w_io = A.bf16(8, 1280)
            w_o = w_io[:, :, 0:1024]
            wglu = A.bf16(4, 512)
            qT = A.bf16(4, 1024)
            kT = A.bf16(1152)
            vtk = A.bf16(9, 2, 66)
            Uflat = A.bf16(4096)
            Ug = Uflat.rearrange("p (g j h) -> p g j h", g=32, j=8)
            Z = A.bf16(32, 128)
            Xb = A.bf16(32, 128)
            attnT = A.bf16(4, 1024)
            yT = A.bf16(4, 1024)
            y2T = A.bf16(4, 1024)
            smat = A.bf16(32, 3, 128)
            xt = [A.f32(D), A.f32(D)]
            tmp = A.f32(D)
            tmp2 = A.f32(D)
            hb = [A.bf16(D)]
            s_sb = A.f32(4, 256)
            p_bf = A.bf16(4, 256)
            pT = A.bf16(4, 2, 128)
            ao = A.f32(512)
            aob = A.bf16(512)
            sgt = A.f32(512)
            y2p = A.f32(512)
            sqb = A.bf16(512)
            sqb2 = A.bf16(D)
            t1 = A.f32(2, 32)
            t2 = A.f32(2, 32)
            ast = A.f32(64)
            ssa = A.f32(8)
            rsa = A.f32(8)
            rss = A.f32(8)
            Ytok = Uflat.rearrange("p (j c) -> p j c", j=8)
            w_in_v = w_in.rearrange("(k p) n -> p k n", p=128)
            w_out_v = w_out.rearrange("(k p) n -> p k n", p=128)
            DMA("sp", smat.rearrange("p g k c -> p (g k c)"), smat_d, ["smat_d"], ["smat"])
            DMA("pool", wglu, w_glu.rearrange("(k p) n -> p k n", p=128), [], ["wglu"])
            MS("dve", vtk, 1.0, [], ["vtk"])
            rot = [0]

            def bank2():
                rot[0] += 1
                return (rot[0] % 2)

            for u in range(nunits):
                seq, half = u // 2, u % 2
                if half == 0:
                    load_modt(1, seq)
                    MS("pool", Xc, 0.0, [], ["Xc"])
                for tt in range(8):
                    norm_transpose(src, u * 1024 + tt * 128, h2T, tt, xt, tmp, hb, "Wsw", srcdep, None, sqb2)
                DMA("pool", w_io, w_in_v, [], ["w_io"])
                for pr in range(4):
                    for st in range(2):
                        b = bank2()
                        for hh, base in ((pr, 0), (pr + 4, 64)):
                            for k in range(8):
                                MM(ps[b][base:base + 64, :], w_io[:, k, hh * 64:(hh + 1) * 64], h2T[:, k, st * 512:(st + 1) * 512],
                                   k == 0, k == 7, ["w_io", "Wsw"], ["ps%d" % b])
                        CP("act", qT[:, pr, st * 512:(st + 1) * 512], ps[b], ["ps%d" % b], ["qT"])
                for st in range(2):
                    b = bank2()
                    for k in range(8):
                        MM(ps[b], w_io[:, k, 512:640], h2T[:, k, st * 512:(st + 1) * 512], k == 0, k == 7, ["w_io", "Wsw"], ["ps%d" % b])
                    CP("act", kT[:, 128 + st * 512:128 + (st + 1) * 512], ps[b], ["ps%d" % b], ["kT"])
                for tt in range(8):
                    b = bank2()
                    for k in range(8):
                        MM(ps[b][:, 0:128], h2T[:, k, tt * 128:(tt + 1) * 128], w_io[:, k, 640:768], k == 0, k == 7, ["w_io", "Wsw"], ["ps%d" % b])
                    CP("dve", vtk[:, 1 + tt, :, 0:64], ps[b][:, 0:128].rearrange("p (a d) -> p a d", a=2), ["ps%d" % b], ["vtk"])
                for j in range(8):
                    b = bank2()
                    for k in range(8):
                        MM(ps[b], h2T[:, k, :].rearrange("p (n j) -> p n j", j=8)[:, :, j], w_io[:, k, 768:1280], k == 0, k == 7,
                           ["w_io", "Wsw"], ["ps%d" % b])
                    CP("act", Ug[:, :, j, :], ps[b].rearrange("p (g h) -> p g h", g=32), ["ps%d" % b], ["U"])
                for gb in range(4):
                    b = 4 + gb % 2
                    pbz = psb[b].rearrange("p (g n) -> p g n", g=8)
                    for gi in range(8):
                        g = gb * 8 + gi
                        TR(pbz[:, gi, :], Ug[:, g].rearrange("p j h -> p (j h)"), ident_b, ["U", "ident_b"], ["ps%d" % b])
                    CP("dve", Z[:, gb * 8:(gb + 1) * 8, :], pbz, ["ps%d" % b], ["Z"])
                for gb in range(8):
                    b = bank2()
                    for gi in range(4):
                        g = gb * 4 + gi
                        MM(ps[b][:, gi * 128:(gi + 1) * 128], smat[:, g, 1, :], Z[:, g, :], True, True, ["smat", "Z"], ["ps%d" % b])
                    CP("act", Wst[:, 0, gb * 4:(gb + 1) * 4, :].rearrange("p g n -> p (g n)"), ps[b], ["ps%d" % b], ["Wx"])
                    b2_ = 2 + gb % 2
                    MM(ps[b2_], perm_f, Wst[:, 0, gb * 4:(gb + 1) * 4, :].rearrange("p g n -> p (g n)"), True, True,
                       ["perm_a", "perm_b", "Wx"], ["ps%d" % b2_])
                    CP("act", Wst[:, 1, gb * 4:(gb + 1) * 4, :].rearrange("p g n -> p (g n)"), ps[b2_], ["ps%d" % b2_], ["Wsw"])
                CP("pool", Xb[:, :, 0], Xc[:, 0, :], ["Xc"], ["Xb0"])

                def chain(n0, n1):
                    for n in range(n0, n1):
                        if n == 0:
                            prev, prev_sw, prev_x = Xc, Xc[:, 1, :], Xc[:, 0, :]
                        else:
                            prev, prev_sw, prev_x = Wst[:, :, :, n - 1], Wst[:, 1, :, n - 1], Wst[:, 0, :, n - 1]
                        TT("pool", t1, AA, prev, ALU.mult, ["AA", "Wx", "Wsw", "Xc"], ["t1"])
                        TT("pool", t2[:, 0, :], BB[:, 0, :], prev_sw, ALU.mult, ["BB", "Wx", "Wsw", "Xc"], ["t2a"])
                        TT("pool", t2[:, 1, :], BB[:, 1, :], prev_x, ALU.mult, ["BB", "Wx", "Wsw", "Xc"], ["t2b"])
                        TT("pool", t1, t1, t2, ALU.add, ["t1", "t2a", "t2b"], ["t1"])
                        TT("pool", Wst[:, :, :, n], Wst[:, :, :, n], t1, ALU.add, ["t1", "Wx", "Wsw"], ["Wx", "Wsw"])

                chain(0, 64)
                CP("pool", Xb[:, :, 1:65], Wst[:, 0, :, 0:64], ["Wx"], ["Xb1h0"])
                DMA("pool", w_o, w_out_v, [], ["w_io"])
                for bi in range(8):
                    first = (half == 0 and bi == 0)
                    for kv in range(2):
                        sb_ = (2 * bi + kv) % 2
                        psS = [ps[2 * sb_], ps[2 * sb_ + 1]]
                        pr_ = slice(kv * 64, (kv + 1) * 64)
                        c0_ = 128 if first else 0
                        for hh in range(4):
                            o_ = psS[hh // 2][:, (hh % 2) * 256 + c0_:(hh % 2) * 256 + 256]
                            MM(o_, qT[pr_, hh, bi * 128:(bi + 1) * 128], kT[pr_, bi * 128 + c0_:bi * 128 + 256], True, True,
                               ["qT", "kT"], ["ps%d" % (2 * sb_ + hh // 2)])
                        for hp in range(2):
                            STT("dve", s_sb[:, 2 * hp:2 * hp + 2, c0_:256], psS[hp].rearrange("p (a b) -> p a b", a=2)[:, :, c0_:256], 0.125,
                                atab[:, kv * 4 + 2 * hp:kv * 4 + 2 * hp + 2, c0_:256], ALU.mult, ALU.add,
                                ["ps%d" % (2 * sb_ + hp), "atab"], ["s_sb%d" % hp])
                        OP("dve", lambda e, c0_=c0_: e.reduce_max(ast[:, 0:4], s_sb[:, :, c0_:256], AX.X), ["s_sb0", "s_sb1"], ["ast_m"])
                        TT("dve", ast[:, 0:4], ast[:, 0:4], sink_t[:, kv * 4:kv * 4 + 4], ALU.max, ["ast_m", "sink_t"], ["ast_m"])
                        TS("dve", ast[:, 4:8], ast[:, 0:4], -1.0, None, ALU.mult, None, ["ast_m"], ["ast_nm"])
                        TT("dve", ast[:, 8:12], sink_t[:, kv * 4:kv * 4 + 4], ast[:, 0:4], ALU.subtract, ["ast_m", "sink_t"], ["ast_d"])
                        ACT(ast[:, 12:16], ast[:, 8:12], AF.Exp, ["ast_d"], ["ast_es"])
                        for hh in range(4):
                            ACT(p_bf[:, hh, c0_:256], s_sb[:, hh, c0_:256], AF.Exp, ["s_sb0", "s_sb1", "ast_nm"], ["p_bf"], bias=ast[:, 4 + hh:5 + hh])
                        tb_ = 4 + sb_
                        pbt = psb[tb_].rearrange("p (h k q) -> p h k q", h=4, k=2)
                        for hh in range(4):
                            for kh in range(1 if first else 0, 2):
                                TR(pbt[:, hh, kh, :], p_bf[:, hh, kh * 128:(kh + 1) * 128], ident_b, ["p_bf", "ident_b"], ["ps%d" % tb_])
                        if first:
                            CP("act", pT[:, :, 1, :], pbt[:, :, 1, :], ["ps%d" % tb_], ["pT"])
                        else:
                            CP("act", pT, pbt, ["ps%d" % tb_], ["pT"])
                        ob_ = 6 + sb_
                        psO = ps[ob_][:, 0:260].rearrange("p (h d) -> p h d", h=4)
                        for hh in range(4):
                            khs = [1] if first else [0, 1]
                            for kh in khs:
                                MM(psO[:, hh, :], pT[:, hh, kh, :], vtk[:, bi + kh, kv, 0:65], kh == khs[0], kh == 1,
                                   ["pT", "vtk"], ["ps%d" % ob_])
                        TT("dve", ast[:, 16:20], psO[:, :, 64], ast[:, 12:16], ALU.add, ["ps%d" % ob_, "ast_es"], ["ast_den"])
                        OP("dve", lambda e: e.reciprocal(ast[:, 20:24], ast[:, 16:20]), ["ast_den"], ["ast_rd"])
                        TT("dve", ao[:, kv * 256:(kv + 1) * 256].rearrange("p (h d) -> p h d", h=4), psO[:, :, 0:64],
                           ast[:, 20:24].unsqueeze(2).broadcast_to([128, 4, 64]), ALU.mult, ["ps%d" % ob_, "ast_rd"], ["ao%d" % kv])
                    MS("dve", ssa[:, bi:bi + 1], 0.0, [], ["ssa%d" % bi])
                    ACT(tmp2[:, 0:512], ao, AF.Square, ["ao0", "ao1", "ssa%d" % bi], ["tmp2", "ssa%d" % bi], accum=ssa[:, bi:bi + 1])
                    TT("dve", aob, ao, gattn_t, ALU.mult, ["ao0", "ao1", "gattn_t"], ["aob"])
                    tb_ = 4 + bi % 2
                    pba = psb[tb_][:, 0:512].rearrange("p (c q) -> p c q", c=4)
                    for cc in range(4):
                        TR(pba[:, cc, :], aob[:, cc * 128:(cc + 1) * 128], ident_b, ["aob", "ident_b"], ["ps%d" % tb_])
                    CP("act", attnT[:, :, bi * 128:(bi + 1) * 128], pba, ["ps%d" % tb_], ["attnT"])
                chain(64, 128)
                CP("pool", Xc, Wst[:, :, :, 127], ["Wx", "Wsw"], ["Xc"])
                CP("pool", Xb[:, :, 65:128], Wst[:, 0, :, 64:127], ["Wx"], ["Xb1h1"])
                CP("dve", kT[:, 0:128], kT[:, 1024:1152], ["kT"], ["kT"])
                CP("dve", vtk[:, 0, :, 0:64], vtk[:, 8, :, 0:64], ["vtk"], ["vtk"])
                ACT(rsa, ssa, AF.Sqrt, ["ssa%d" % i_ for i_ in range(8)] + ["cst0"], ["rsa"], bias=eps_c, scale=1.0 / 512)
                OP("dve", lambda e: e.reciprocal(rsa, rsa), ["rsa"], ["rsa"])
                for hf_ in range(2):
                    pp = slice(hf_ * 64, (hf_ + 1) * 64)
                    hn = "_h%d" % hf_
                    for gb in range(8):
                        b = bank2()
                        for gi in range(4):
                            g = gb * 4 + gi
                            o_ = ps[b][pp, gi * 128:(gi + 1) * 128]
                            MM(o_, Z[:, g, hf_ * 64:(hf_ + 1) * 64], smat[:, g, 0, :], True, False, ["Z", "smat"], ["ps%d" % b])
                            MM(o_, Xb[:, g, hf_ * 64:(hf_ + 1) * 64], smat[:, g, 2, :], False, True, ["Xb0", "Xb1" + ("h%d" % hf_), "smat"], ["ps%d" % b])
                        ACT(Ytok[pp, :, gb * 64:(gb + 1) * 64].rearrange("p j (g h) -> p g j h", g=4),
                            ps[b][pp, :].rearrange("p (g j h) -> p g j h", g=4, j=8), AF.Gelu_apprx_tanh, ["ps%d" % b], ["U" + hn, "U"])
                    for cc in range(4):
                        b = 4 + cc % 2
                        pby = psb[b][:, 0:512].rearrange("p (j n) -> p j n", j=8)
                        for j in range(8):
                            TR(pby[:, j, :], Ytok[pp, j, cc * 128:(cc + 1) * 128], ident_b[pp, pp], ["U" + hn, "ident_b"], ["ps%d" % b])
                        CP("dve", yT[:, cc, hf_ * 512:(hf_ + 1) * 512].rearrange("p (n j) -> p j n", j=8), pby, ["ps%d" % b], ["yT" + hn])
                    st = hf_
                    pcol = 300 + 40 * hf_
                    pstat32 = ps[7][:, pcol:pcol + 16].rearrange("p (o t) -> p o t", o=4)
                    for oc in range(4):
                        b = bank2()
                        for kc in range(4):
                            MM(ps[b], wglu[:, kc, oc * 128:(oc + 1) * 128], yT[:, kc, st * 512:(st + 1) * 512], kc == 0, kc == 3,
                               ["wglu", "yT" + hn], ["ps%d" % b])
                        ACT(sgt, ps[b], AF.Sigmoid, ["ps%d" % b, "bglu_c"], ["sgt"], bias=bglu_c[:, oc:oc + 1])
                        TT("dve", y2p, yT[:, oc, st * 512:(st + 1) * 512], sgt, ALU.mult, ["yT" + hn, "sgt"], ["y2p"])
                        ACT(sqb, y2p, AF.Square, ["y2p"], ["sqb"])
                        TS("dve", y2T[:, oc, st * 512:(st + 1) * 512], y2p, gssm_c[:, oc:oc + 1], None, ALU.mult, None, ["y2p", "gssm_c"], ["y2T" + hn])
                        for t4 in range(4):
                            MM(pstat32[:, oc, t4:t4 + 1], sqb[:, t4 * 128:(t4 + 1) * 128], ones_b[:, 0:1], True, True,
                               ["sqb", "ones_b"], ["pstat" + hn, "ps7"])
                    rssh = rss[:, hf_ * 4:(hf_ + 1) * 4]
                    OP("dve", lambda e, pcol=pcol, rssh=rssh: e.reduce_sum(rssh, ps[7][:, pcol:pcol + 16].rearrange("p (o t) -> p t o", o=4), AX.X),
                       ["pstat" + hn, "ps7"], ["rss" + hn])
                    ACT(rssh, rssh, AF.Sqrt, ["rss" + hn, "cst0"], ["rss" + hn], bias=eps_c, scale=1.0 / 512)
                    OP("dve", lambda e, rssh=rssh: e.reciprocal(rssh, rssh), ["rss" + hn], ["rss" + hn])
                    for tt in range(hf_ * 4, hf_ * 4 + 4):
                        pa = [0, 1] if tt % 2 == 0 else [2, 3]
                        for hf in range(2):
                            for kc in range(4):
                                MM(ps[pa[hf]], attnT[:, kc, tt * 128:(tt + 1) * 128], w_o[:, kc, hf * 512:(hf + 1) * 512], kc == 0, kc == 3,
                                   ["attnT", "w_io"], ["ps%d" % pa[hf]])
                        for hf in range(2):
                            for kc in range(4):
                                MM(ps[4 + hf], y2T[:, kc, tt * 128:(tt + 1) * 128], w_o[:, 4 + kc, hf * 512:(hf + 1) * 512], kc == 0, kc == 3,
                                   ["y2T" + hn, "w_io"], ["ps%d" % (4 + hf)])
                        for hf in range(2):
                            sl = slice(hf * 512, (hf + 1) * 512)
                            TS("dve", tmp2[:, sl], ps[pa[hf]], rsa[:, tt:tt + 1], None, ALU.mult, None, ["ps%d" % pa[hf], "rsa"], ["tmp2"])
                            STT("dve", tmp2[:, sl], ps[4 + hf], rss[:, tt:tt + 1], tmp2[:, sl], ALU.mult, ALU.add,
                                ["ps%d" % (4 + hf), "rss" + hn, "tmp2"], ["tmp2"])
                        b = tt % 2
                        DMA("sp", xt[b], src[u * 1024 + tt * 128:u * 1024 + (tt + 1) * 128, :], list(srcdep), ["xt%d" % b])
                        MS("dve", stat[:, 4:5], 0.0, [], ["st4"])
                        ACT(sqb2, tmp2, AF.Square, ["tmp2", "st4"], ["st4"], accum=stat[:, 4:5])
                        ACT(stat[:, 7:8], stat[:, 4:5], AF.Sqrt, ["st4", "cst0"], ["st7"], bias=eps_c, scale=1.0 / D)
                        OP("dve", lambda e: e.reciprocal(stat[:, 8:9], stat[:, 7:8]), ["st7"], ["st8"])
                        STT("dve", tmp, tmp2, stat[:, 8:9], modt[:, 2, :], ALU.mult, ALU.mult, ["tmp2", "st8", "modt", "tmp"], ["tmp"])
                        TT("pool", xt[b], tmp, xt[b], ALU.add, ["tmp", "xt%d" % b], ["xt%d" % b])
                        DMA("sp", dst[u * 1024 + tt * 128:u * 1024 + (tt + 1) * 128, :], xt[b], ["xt%d" % b], ["dst"])
            A.pop()
            P.barrier()

        ffn_phase(0, x, x1d if stage >= 2 else out)
        if stage >= 2:
            mixer_phase(x1d, x2d if stage >= 3 else out, ["dst"])
        if stage >= 3:
            ffn_phase(2, x2d, out, ["dst"])

        OP("sp", None, ["dst", "tabs"], [])
        P.emit()
        nc._prog_stats = P.stats
    return nc


_INPUT_ORDER = None


def kernel(**inputs):
    stage = int(inputs.pop("_stage", 3)) if "_stage" in inputs else 3
    f = lambda a: np.ascontiguousarray(np.asarray(a, dtype=np.float32))
    x = f(inputs["x"])
    c = f(inputs["c"])
    shared = {}
    for n in ("w_ada", "b_ada", "g_pre_ff1", "g_post_ff1", "w1_gate", "w1_up", "w1_down", "g_pre_mix", "g_post_mix",
              "w_in", "attn_sinks", "ssm_a_re", "ssm_a_im", "ssm_log_step", "ssm_b_re", "ssm_b_im", "ssm_c_re",
              "ssm_c_im", "ssm_d", "ssm_w_glu", "ssm_b_glu", "g_attn_out", "g_ssm_out", "w_out", "g_pre_ff2",
              "g_post_ff2", "w2_gate", "w2_up", "w2_down"):
        a = f(inputs[n])[0]
        if n in ("ssm_c_re", "ssm_c_im"):
            a = np.ascontiguousarray(a.reshape(512, 64))
        shared[n] = a
    nc = build(stage)
    in_maps = []
    for i in range(8):
        m = dict(shared)
        m["x"] = np.ascontiguousarray(x[2 * i:2 * i + 2].reshape(TOK, D))
        m["c"] = np.ascontiguousarray(c[2 * i:2 * i + 2])
        in_maps.append(m)
    res = run_bass_kernel_spmd(nc, in_maps, core_ids=list(range(8)))
    outs = [np.asarray(r["out"]).reshape(2, 2048, D) for r in res.results]
    return np.concatenate(outs, axis=0).astype(np.float32)
```

```python
import contextlib
import math
import numpy as np
import concourse.bass as bass
import concourse.mybir as mybir
from concourse.bass_utils import run_bass_kernel_spmd

F32 = mybir.dt.float32
BF16 = mybir.dt.bfloat16
ALU = mybir.AluOpType
AF = mybir.ActivationFunctionType
AX = mybir.AxisListType

D = 1024
DFF = 2816
NCH = 22
TOK = 4096
EPS = 1e-6


class _Op:
    __slots__ = ("eng", "fn", "deps", "dma", "signal", "token", "prev_tok", "idx")


class Prog:
    ENGS = ("pe", "act", "dve", "pool", "sp")
    NDMA = {"sp": 4, "pool": 4, "act": 2}

    def __init__(self, nc, same_engine_sync=True):
        self.nc = nc
        self.ops = []
        self.last_w = {}
        self.readers = {}
        self.same_engine_sync = same_engine_sync
        self.bar = set()
        self.bar_start = 0

    def op(self, eng, fn, reads=(), writes=(), dma=False):
        o = _Op()
        o.eng, o.fn, o.dma = eng, fn, dma
        o.idx = len(self.ops)
        deps = set()
        for r in reads:
            w = self.last_w.get(r)
            if w is not None:
                deps.add(w)
        for r in writes:
            w = self.last_w.get(r)
            if w is not None:
                deps.add(w)
            for rd in self.readers.get(r, ()):
                deps.add(rd)
        for r in reads:
            self.readers.setdefault(r, []).append(o.idx)
        for r in writes:
            self.last_w[r] = o.idx
            self.readers[r] = []
        deps.discard(o.idx)
        fd = set(self.bar)
        for d in deps:
            do = self.ops[d]
            if do.eng == eng and not do.dma:
                if eng == "pe" or eng == "sp" or not self.same_engine_sync:
                    continue
            fd.add(d)
        o.deps = fd
        o.signal = dma
        o.token = None
        o.prev_tok = None
        self.ops.append(o)
        return o.idx

    def barrier(self):
        last = {}
        nb = set()
        for o in self.ops[self.bar_start:]:
            if o.dma:
                nb.add(o.idx)
            elif o.fn is not None:
                last[o.eng] = o.idx
        nb.update(last.values())
        self.bar = set(self.bar) | nb
        self.bar_start = len(self.ops)

    def emit(self):
        nc = self.nc
        ops = self.ops
        for o in ops:
            for d in o.deps:
                ops[d].signal = True
        with contextlib.ExitStack() as es:
            esem = {e: es.enter_context(nc.semaphore("s_" + e)) for e in self.ENGS}
            dsem = {
                e: [es.enter_context(nc.semaphore("d_%s%d" % (e, i))) for i in range(self.NDMA[e])]
                for e in ("sp", "pool", "act")
            }
            ecnt = {e: 0 for e in self.ENGS}
            dcnt = {e: 0 for e in dsem}
            for o in ops:
                if o.dma:
                    i = dcnt[o.eng]
                    dcnt[o.eng] += 1
                    nd = self.NDMA[o.eng]
                    s = dsem[o.eng][i % nd]
                    o.token = (s, 16 * (i // nd + 1))
                    if i >= nd:
                        o.prev_tok = (s, 16 * (i // nd))
                elif o.signal:
                    ecnt[o.eng] += 1
                    o.token = (esem[o.eng], ecnt[o.eng])
            assert max(ecnt.values()) < 60000, ecnt
            self.stats = dict(ecnt=ecnt, dcnt=dcnt, nops=len(ops))
            block = es.enter_context(nc.Block())

            def run(engname, e):
                known = {}
                for o in ops:
                    if o.eng != engname:
                        continue
                    toks = [ops[d].token for d in sorted(o.deps)]
                    if o.prev_tok is not None:
                        toks.append(o.prev_tok)
                    need = {}
                    for s, v in toks:
                        k = id(s)
                        if known.get(k, 0) >= v:
                            continue
                        if k not in need or need[k][1] < v:
                            need[k] = (s, v)
                    for k, (s, v) in need.items():
                        e.wait_ge(s, v)
                        known[k] = v
                    if o.fn is None:
                        continue
                    inst = o.fn(e)
                    if o.token is not None:
                        inst.then_inc(o.token[0], 16 if o.dma else 1)

            @block.tensor
            def _(e):
                run("pe", e)

            @block.scalar
            def _(e):
                run("act", e)

            @block.vector
            def _(e):
                run("dve", e)

            @block.gpsimd
            def _(e):
                run("pool", e)

            @block.sync
            def _(e):
                run("sp", e)


class Arena:
    def __init__(self, t, nwords):
        self.t = t
        self.n = nwords
        self.off = 0
        self.marks = []

    def f32(self, *shape):
        n = int(np.prod(shape))
        assert self.off + n <= self.n, ("arena overflow", self.off, n, self.n)
        ap = self.t[:, self.off:self.off + n]
        self.off += n
        return self._shape(ap, shape)

    def bf16(self, *shape):
        n = int(np.prod(shape))
        w = (n + 1) // 2
        assert self.off + w <= self.n, ("arena overflow", self.off, w, self.n)
        ap = self.t[:, self.off:self.off + w].bitcast(BF16)
        if 2 * w != n:
            ap = ap[:, 0:n]
        self.off += w
        return self._shape(ap, shape)

    @staticmethod
    def _shape(ap, shape):
        if len(shape) == 1:
            return ap
        if len(shape) == 2:
            return ap.rearrange("p (a b) -> p a b", a=shape[0])
        if len(shape) == 3:
            return ap.rearrange("p (a b c) -> p a b c", a=shape[0], b=shape[1])
        raise ValueError(shape)

    def push(self):
        self.marks.append(self.off)

    def pop(self):
        self.off = self.marks.pop()


def build(stage=3, nunits=4):
    nc = bass.Bass("TRN2", target_bir_lowering=False)

    def din(name, shape):
        return nc.dram_tensor(name, list(shape), F32, kind="ExternalInput").ap()

    x = din("x", [TOK, D])
    c = din("c", [2, D])
    w_ada = din("w_ada", [D, 9 * D])
    b_ada = din("b_ada", [9 * D])
    gvec = {n: din(n, [D]) for n in ("g_pre_ff1", "g_post_ff1", "g_pre_mix", "g_post_mix", "g_pre_ff2", "g_post_ff2")}
    wff = {}
    for i in (1, 2):
        wff[i] = (din("w%d_gate" % i, [D, DFF]), din("w%d_up" % i, [D, DFF]), din("w%d_down" % i, [DFF, D]))
    w_in = din("w_in", [D, 1280])
    sinks = din("attn_sinks", [8])
    a_re = din("ssm_a_re", [32, 64])
    a_im = din("ssm_a_im", [32, 64])
    log_step = din("ssm_log_step", [32])
    b_re = din("ssm_b_re", [32, 64, 16])
    b_im = din("ssm_b_im", [32, 64, 16])
    c_re = din("ssm_c_re", [512, 64])
    c_im = din("ssm_c_im", [512, 64])
    ssm_d = din("ssm_d", [512])
    w_glu = din("ssm_w_glu", [512, 512])
    b_glu = din("ssm_b_glu", [512])
    g_attn = din("g_attn_out", [512])
    g_ssm = din("g_ssm_out", [512])
    w_out = din("w_out", [D, D])
    out = nc.dram_tensor("out", [TOK, D], F32, kind="ExternalOutput").ap()
    x1d = nc.dram_tensor("x1_scr", [TOK, D], F32).ap()
    x2d = nc.dram_tensor("x2_scr", [TOK, D], F32).ap()
    tabs = nc.dram_tensor("tabs_scr", [2, 9 * D], F32).ap()
    smat_d = nc.dram_tensor("smat_scr", [128, 32 * 3 * 128], BF16).ap()

    NW = 53000
    with contextlib.ExitStack() as es:
        arena_t = es.enter_context(nc.sbuf_tensor("arena", [128, NW], F32))
        pst = [es.enter_context(nc.psum_tensor("ps%d" % i, [128, 512], F32)) for i in range(8)]
        ps = [t[:, :] for t in pst]
        psb = [t[:, :].bitcast(BF16) for t in pst]
        A = Arena(arena_t, NW)
        P = Prog(nc)

        def OP(eng, fn, r=(), w=(), dma=False):
            return P.op(eng, fn, r, w, dma)

        def MM(o, l, rr, st, sp_, r, w):
            OP("pe", lambda e: e.matmul(o, l, rr, start=st, stop=sp_), r, w)

        def TR(o, i, idn, r, w):
            OP("pe", lambda e: e.transpose(o, i, idn), r, w)

        def ACT(o, i, func, r, w, bias=None, scale=None, accum=None):
            kw = {}
            if bias is not None:
                kw["bias"] = bias
            if scale is not None:
                kw["scale"] = scale
            if accum is not None:
                kw["accum_out"] = accum
            OP("act", lambda e: e.activation(o, i, func, **kw), r, w)

        def TT(eng, o, a, b, op, r, w):
            OP(eng, lambda e: e.tensor_tensor(o, a, b, op), r, w)

        def TS(eng, o, a, s1, s2, op0, op1, r, w):
            if s2 is None:
                OP(eng, lambda e: e.tensor_scalar(o, a, s1, None, op0), r, w)
            else:
                OP(eng, lambda e: e.tensor_scalar(o, a, s1, s2, op0, op1), r, w)

        def STT(eng, o, a, s, b, op0, op1, r, w):
            OP(eng, lambda e: e.scalar_tensor_tensor(o, a, s, b, op0, op1), r, w)

        def CP(eng, o, i, r, w):
            if eng == "act":
                OP("act", lambda e: e.activation(o, i, AF.Copy), r, w)
            else:
                OP(eng, lambda e: e.tensor_copy(o, i), r, w)

        def MS(eng, o, val, r, w):
            OP(eng, lambda e: e.memset(o, val), r, w)

        def DMA(q, o, i, r, w, slow=False):
            if slow:
                OP(q, lambda e: e.dma_start(out=o, in_=i, allow_slow_non_contiguous=True), r, w, dma=True)
            else:
                OP(q, lambda e: e.dma_start(out=o, in_=i), r, w, dma=True)

        ident_b = A.bf16(128)
        ident_f = A.f32(128)
        perm_f = A.f32(128)
        ones_b = A.bf16(2)
        cst = A.f32(8)
        MS("pool", ident_f, 0.0, [], ["ident_f"])
        OP("pool", lambda e: e.affine_select(out=ident_f, in_=ident_f, compare_op=ALU.not_equal, fill=1.0, base=0,
                                             pattern=[[-1, 128]], channel_multiplier=1), ["ident_f"], ["ident_f"])
        CP("dve", ident_b, ident_f, ["ident_f"], ["ident_b"])
        CP("dve", perm_f[:, 0:64], ident_f[:, 64:128], ["ident_f"], ["perm_a"])
        CP("dve", perm_f[:, 64:128], ident_f[:, 0:64], ["ident_f"], ["perm_b"])
        MS("dve", ones_b, 1.0, [], ["ones_b"])
        MS("dve", cst[:, 0:1], EPS, [], ["cst0"])
        MS("dve", cst[:, 1:2], math.pi / 2, [], ["cst1"])
        MS("dve", cst[0:64, 2:3], 1.0, [], ["cst2a"])
        MS("dve", cst[64:128, 2:3], 0.0, [], ["cst2b"])
        MS("dve", cst[0:64, 3:4], 0.0, [], ["cst3a"])
        MS("dve", cst[64:128, 3:4], 1.0, [], ["cst3b"])
        MS("dve", cst[0:64, 4:5], -1.0, [], ["cst4a"])
        MS("dve", cst[64:128, 4:5], 1.0, [], ["cst4b"])
        CST = ["cst0", "cst1", "cst2a", "cst2b", "cst3a", "cst3b", "cst4a", "cst4b"]
        eps_c = cst[:, 0:1]

        AA = A.f32(2, 32)
        BB = A.f32(2, 32)
        Xc = A.f32(2, 32)
        atab = A.f32(8, 256)
        sink_t = A.f32(8)
        gattn_t = A.f32(512)
        gssm_c = A.f32(4)
        bglu_c = A.f32(4)
        modt_flat = A.f32(3 * D)
        modt = modt_flat.rearrange("p (a b) -> p a b", a=3)
        stat = A.f32(64)

        A.push()
        cT = A.f32(8, 2)
        wb = [A.f32(8, 512) for _ in range(4)]
        modr = arena_t[0:2, A.off:A.off + 9 * D]
        A.off += 9 * D
        b2 = arena_t[0:2, A.off:A.off + 9 * D]
        A.off += 9 * D
        g6 = arena_t[0:2, A.off:A.off + 6 * D]
        A.off += 6 * D
        for bb in range(2):
            DMA("sp", cT[:, :, bb], c[bb].rearrange("(k p) -> p k", p=128), [], ["cT"], slow=True)
        DMA("sp", b2, b_ada.partition_broadcast(2), [], ["b2"])
        for i, n in enumerate(("g_pre_ff1", "g_post_ff1", "g_pre_mix", "g_post_mix", "g_pre_ff2", "g_post_ff2")):
            DMA("sp", g6[:, i * D:(i + 1) * D], gvec[n].partition_broadcast(2), [], ["g6_%d" % i])
        ACT(cT, cT, AF.Silu, ["cT"], ["cT"])
        w_ada_v = w_ada.rearrange("(k p) n -> p k n", p=128)
        for cb in range(18):
            b = cb % 2
            wq = cb % 4
            DMA("sp", wb[wq], w_ada_v[:, :, cb * 512:(cb + 1) * 512], [], ["wb%d" % wq])
            for k in range(8):
                MM(ps[b][0:2, :], cT[:, k, :], wb[wq][:, k, :], k == 0, k == 7, ["cT", "wb%d" % wq], ["ps%d" % b])
            TT("dve", modr[:, cb * 512:(cb + 1) * 512], ps[b][0:2, :], b2[:, cb * 512:(cb + 1) * 512], ALU.add,
               ["ps%d" % b, "b2"], ["modr%d" % cb])
        for i in range(3):
            sc = modr[:, (3 * i + 1) * D:(3 * i + 2) * D]
            ga = modr[:, (3 * i + 2) * D:(3 * i + 3) * D]
            STT("dve", sc, sc, 1.0, g6[:, (2 * i) * D:(2 * i + 1) * D], ALU.add, ALU.mult,
                ["modr%d" % cb_ for cb_ in range(18)] + ["g6_%d" % (2 * i)], ["modr"])
            STT("dve", ga, ga, 0.5 if i != 1 else 1.0, g6[:, (2 * i + 1) * D:(2 * i + 2) * D], ALU.mult, ALU.mult,
                ["modr", "g6_%d" % (2 * i + 1)], ["modr"])
        DMA("sp", tabs, modr, ["modr"] + ["modr%d" % cb_ for cb_ in range(18)], ["tabs"])
        A.pop()
        P.barrier()

        def load_modt(i, seq):
            src = tabs[seq, 3 * i * D:(3 * i + 3) * D].partition_broadcast(128)
            DMA("sp", modt_flat, src, ["tabs"], ["modt"])

        def norm_transpose(src, r0, hT, tt, xt, tmp, hb, tagp, srcdep=(), mt=None, junk=None, part="all"):
            if part == "all":
                for p_ in ("norm", "tr", "copy"):
                    norm_transpose(src, r0, hT, tt, xt, tmp, hb, tagp, srcdep, mt, junk, p_)
                return
            if mt is None:
                mt = (modt, "modt")
            mtab, mname = mt
            b = tt % 2
            c0 = 10 * b
            sq_out = junk if junk is not None else tmp
            sq_w = [] if junk is not None else ["tmp"]
            hi = b % len(hb)
            pb = psb[4 + b].rearrange("p (k n) -> p k n", k=8)
            if part == "tr":
                for k in range(8):
                    TR(pb[:, k, :], hb[hi][:, k * 128:(k + 1) * 128], ident_b, ["hb%d" % hi, "ident_b"], ["ps%d" % (4 + b)])
                return
            if part == "copy":
                CP("act", hT[:, :, tt * 128:(tt + 1) * 128], pb, ["ps%d" % (4 + b)], [tagp])
                return
            DMA("sp", xt[b], src[r0:r0 + 128, :], list(srcdep), ["xt%d" % b])
            MS("dve", stat[:, c0:c0 + 1], 0.0, [], ["st0_%d" % b])
            ACT(sq_out, xt[b], AF.Square, ["xt%d" % b, "st0_%d" % b], sq_w + ["st0_%d" % b], accum=stat[:, c0:c0 + 1])
            ACT(stat[:, c0 + 1:c0 + 2], stat[:, c0:c0 + 1], AF.Sqrt, ["st0_%d" % b, "cst0"], ["st1_%d" % b], bias=eps_c, scale=1.0 / D)
            OP("dve", lambda e: e.reciprocal(stat[:, c0 + 2:c0 + 3], stat[:, c0 + 1:c0 + 2]), ["st1_%d" % b], ["st2_%d" % b])
            STT("dve", tmp, xt[b], stat[:, c0 + 2:c0 + 3], mtab[:, 1, :], ALU.mult, ALU.mult, ["xt%d" % b, "st2_%d" % b, mname, "tmp"], ["tmp"])
            TT("pool", hb[hi], tmp, mtab[:, 0, :], ALU.add, ["tmp", mname], ["hb%d" % hi])

        def residual_out(src, dst, r0, b, xr, tmp, pss, srcdep=(), mt=None, junk=None):
            if mt is None:
                mt = (modt, "modt")
            mtab, mname = mt
            c0 = 4 + 10 * b
            DMA("sp", xr[b], src[r0:r0 + 128, :], list(srcdep), ["xr%d" % b])
            MS("dve", stat[:, c0:c0 + 2], 0.0, [], ["st4_%d" % b])
            for hf in range(2):
                ACT(junk[:, hf * 512:(hf + 1) * 512], ps[pss[hf]], AF.Square, ["ps%d" % pss[hf], "st4_%d" % b], ["st4_%d" % b],
                    accum=stat[:, c0 + hf:c0 + hf + 1])
            TT("dve", stat[:, c0 + 2:c0 + 3], stat[:, c0:c0 + 1], stat[:, c0 + 1:c0 + 2], ALU.add, ["st4_%d" % b], ["st6_%d" % b])
            ACT(stat[:, c0 + 3:c0 + 4], stat[:, c0 + 2:c0 + 3], AF.Sqrt, ["st6_%d" % b, "cst0"], ["st7_%d" % b], bias=eps_c, scale=1.0 / D)
            OP("dve", lambda e: e.reciprocal(stat[:, c0 + 4:c0 + 5], stat[:, c0 + 3:c0 + 4]), ["st7_%d" % b], ["st8_%d" % b])
            for hf in range(2):
                STT("dve", tmp[:, hf * 512:(hf + 1) * 512], ps[pss[hf]], stat[:, c0 + 4:c0 + 5], mtab[:, 2, hf * 512:(hf + 1) * 512],
                    ALU.mult, ALU.mult, ["ps%d" % pss[hf], "st8_%d" % b, mname, "tmpC"], ["tmpC"])
            TT("pool", xr[b], tmp, xr[b], ALU.add, ["tmpC", "xr%d" % b], ["xr%d" % b])
            DMA("sp", dst[r0:r0 + 128, :], xr[b], ["xr%d" % b], ["dst"])

        def ffn_phase(i, src, dst, srcdep=()):
            wg_d, wu_d, wd_d = wff[1 if i == 0 else 2]
            A.push()
            hT2 = [A.bf16(8, 1024), A.bf16(8, 1024)]
            actT = A.bf16(NCH, 1024)
            wg = [A.bf16(8, 256), A.bf16(8, 256)]
            wu = [A.bf16(8, 256), A.bf16(8, 256)]
            wdn = A.bf16(NCH, 1024)
            xt = [A.f32(D), A.f32(D)]
            xr = [A.f32(D), A.f32(D)]
            hb = [A.bf16(D), A.bf16(D)]
            sg = [A.f32(512), A.f32(512)]
            tmpA = A.f32(D)
            tmpC = A.f32(D)
            junk = A.bf16(D)
            malt_flat = A.f32(3 * D)
            malt = malt_flat.rearrange("p (a b) -> p a b", a=3)
            mts = [(modt, "modt"), (malt, "malt")]
            wg_v = wg_d.rearrange("(k p) n -> p k n", p=128)
            wu_v = wu_d.rearrange("(k p) n -> p k n", p=128)
            wd_v = wd_d.rearrange("(c p) n -> p c n", p=128)
            load_modt(i, 0)
            DMA("sp", malt_flat, tabs[1, 3 * i * D:(3 * i + 3) * D].partition_broadcast(128), ["tabs"], ["malt"])
            for cq in range(2):
                DMA("pool", wdn[:, cq * 11:(cq + 1) * 11, :], wd_v[:, cq * 11:(cq + 1) * 11, :], [], ["wdn%d" % cq])

            def a_tile(u, tt, part="all"):
                norm_transpose(src, u * 1024 + tt * 128, hT2[u % 2], tt, xt, tmpA, hb, "hT%d" % (u % 2), srcdep, mts[u // 2], junk, part)

            for tt in range(8):
                a_tile(0, tt)
            for u in range(nunits):
                hT = hT2[u % 2]
                hTn = "hT%d" % (u % 2)
                j = 0
                for cp_ in range(11):
                    b = cp_ % 2
                    DMA("pool", wg[b], wg_v[:, :, cp_ * 256:(cp_ + 1) * 256], [], ["wg%d" % b])
                    DMA("pool", wu[b], wu_v[:, :, cp_ * 256:(cp_ + 1) * 256], [], ["wu%d" % b])
                    for cl in range(2):
                        ch = cp_ * 2 + cl
                        for st in range(2):
                            pg, pu = (j % 2) * 2, (j % 2) * 2 + 1
                            for k in range(8):
                                MM(ps[pg], wg[b][:, k, cl * 128:(cl + 1) * 128], hT[:, k, st * 512:(st + 1) * 512], k == 0, k == 7,
                                   ["wg%d" % b, hTn], ["ps%d" % pg])
                            for k in range(8):
                                MM(ps[pu], wu[b][:, k, cl * 128:(cl + 1) * 128], hT[:, k, st * 512:(st + 1) * 512], k == 0, k == 7,
                                   ["wu%d" % b, hTn], ["ps%d" % pu])
                            ACT(sg[j % 2], ps[pg], AF.Silu, ["ps%d" % pg], ["sg%d" % (j % 2)])
                            TT("dve", actT[:, ch, st * 512:(st + 1) * 512], sg[j % 2], ps[pu], ALU.mult,
                               ["sg%d" % (j % 2), "ps%d" % pu], ["actT"])
                            j += 1
                for tt in range(8):
                    nxt = u + 1 < nunits
                    if nxt:
                        a_tile(u + 1, tt, "norm")
                    pss = [[0, 1], [2, 3], [6, 7]][tt % 3]
                    for hf in range(2):
                        for ch in range(NCH):
                            MM(ps[pss[hf]], actT[:, ch, tt * 128:(tt + 1) * 128], wdn[:, ch, hf * 512:(hf + 1) * 512],
                               ch == 0, ch == NCH - 1, ["actT", "wdn0", "wdn1"], ["ps%d" % pss[hf]])
                    if nxt:
                        a_tile(u + 1, tt, "tr")
                    residual_out(src, dst, u * 1024 + tt * 128, tt % 2, xr, tmpC, pss, srcdep, mts[u // 2], junk)
                    if nxt:
                        a_tile(u + 1, tt, "copy")
            A.pop()
            P.barrier()

        def setup_mixer():
            MS("dve", cst[0:64, 5:6], -1.0, [], ["cst5a"])
            MS("dve", cst[64:128, 5:6], 0.0, [], ["cst5b"])
            MS("dve", cst[0:64, 6:7], 0.0, [], ["cst6a"])
            MS("dve", cst[64:128, 6:7], -1.0, [], ["cst6b"])
            selLo, selHi, sgn, nselLo, nselHi = cst[:, 2:3], cst[:, 3:4], cst[:, 4:5], cst[:, 5:6], cst[:, 6:7]
            DMA("sp", sink_t, sinks.partition_broadcast(128), [], ["sink_t"])
            DMA("sp", gattn_t, g_attn.partition_broadcast(128), [], ["gattn_t"])
            DMA("sp", gssm_c, g_ssm.rearrange("(c p) -> p c", p=128), [], ["gssm_c"], slow=True)
            DMA("sp", bglu_c, b_glu.rearrange("(c p) -> p c", p=128), [], ["bglu_c"], slow=True)
            MS("dve", Xc, 0.0, [], ["Xc"])
            A.push()
            dist = A.f32(256)
            v1 = A.f32(256)
            v2 = A.f32(256)
            OP("pool", lambda e: e.iota(dist, [[-1, 256]], base=128, channel_multiplier=1,
                                        allow_small_or_imprecise_dtypes=True), [], ["dist"])
            TS("dve", v1, dist, 0.0, None, ALU.is_ge, None, ["dist"], ["v1"])
            TS("dve", v2, dist, 128.0, None, ALU.is_lt, None, ["dist"], ["v2"])
            TT("dve", v1, v1, v2, ALU.mult, ["v1", "v2"], ["v1"])
            TS("dve", v2, v1, 30000.0, -30000.0, ALU.mult, ALU.add, ["v1"], ["v2"])
            for h in range(8):
                STT("dve", atab[:, h, :], dist, -(2.0 ** -(h + 1)), v1, ALU.mult, ALU.mult, ["dist", "v1"], ["atab"])
                TT("dve", atab[:, h, :], atab[:, h, :], v2, ALU.add, ["atab", "v2"], ["atab"])
            are = A.f32(32)
            aim = A.f32(32)
            lsb = A.f32(32)
            Bre = A.f32(32, 16)
            Bim = A.f32(32, 16)
            Cre = A.f32(32, 16)
            Cim = A.f32(32, 16)
            dcol = A.f32(32)
            mask = A.f32(128)
            for hf in range(2):
                sl = slice(hf * 64, (hf + 1) * 64)
                DMA("sp", are[sl, :], a_re.rearrange("g p -> p g"), [], ["are%d" % hf], slow=True)
                DMA("sp", aim[sl, :], a_im.rearrange("g p -> p g"), [], ["aim%d" % hf], slow=True)
                DMA("sp", Bre[sl], b_re.rearrange("g p h -> p g h"), [], ["Bre%d" % hf], slow=True)
                DMA("sp", Bim[sl], b_im.rearrange("g p h -> p g h"), [], ["Bim%d" % hf], slow=True)
            DMA("sp", lsb, log_step.partition_broadcast(128), [], ["lsb"])
            for j in range(8):
                DMA("sp", dcol[j * 16:(j + 1) * 16, :], ssm_d.rearrange("(g h) -> h g", h=16), [], ["dcol%d" % j], slow=True)
            DCOL = ["dcol%d" % j for j in range(8)]
            for nm, src_c, dstC in (("re", c_re, Cre), ("im", c_im, Cim)):
                for t in range(4):
                    cn = A.f32(2, 64)
                    for dd in range(2):
                        DMA("sp", cn[:, dd, :], src_c[t * 128:(t + 1) * 128, :], [], ["cn%s%d_%d" % (nm, t, dd)])
                    pb_ = ps[t % 2][:, 0:128]
                    TR(pb_, cn.rearrange("p a b -> p (a b)"), ident_f, ["cn%s%d_0" % (nm, t), "cn%s%d_1" % (nm, t), "ident_f"],
                       ["ps%d" % (t % 2)])
                    CP("act", dstC[:, t * 8:(t + 1) * 8, :].rearrange("p g h -> p (g h)"), pb_, ["ps%d" % (t % 2)], ["C" + nm])
            MS("dve", mask, 1.0, [], ["mask"])
            OP("pool", lambda e: e.affine_select(out=mask.rearrange("p (a b) -> p a b", a=8), in_=mask.rearrange("p (a b) -> p a b", a=8),
                                                 compare_op=ALU.is_ge, fill=0.0, base=15, pattern=[[16, 8], [0, 16]],
                                                 channel_multiplier=-1), ["mask"], ["mask"])
            cnt = [0]

            def T32():
                cnt[0] += 1
                return A.f32(32), "t32_%d" % cnt[0]

            def mul(a, b):
                o, n = T32()
                TT("dve", o, a[0], b[0], ALU.mult, [a[1], b[1]], [n])
                return (o, n)

            def add(a, b, op=ALU.add):
                o, n = T32()
                TT("dve", o, a[0], b[0], op, [a[1], b[1]], [n])
                return (o, n)

            def cmul(a, b):
                rr = add(mul(a[0], b[0]), mul(a[1], b[1]), ALU.subtract)
                ii = add(mul(a[0], b[1]), mul(a[1], b[0]))
                return (rr, ii)

            AR = (are, "are"); AI = (aim, "aim")
            CP("dve", are, are, ["are0", "are1"], ["are"])
            CP("dve", aim, aim, ["aim0", "aim1"], ["aim"])
            dt_, n_dt = T32()
            ACT(dt_, lsb, AF.Exp, ["lsb"], [n_dt])
            DT = (dt_, n_dt)
            XR = mul(DT, AR)
            XI = mul(DT, AI)
            s0, n_s0 = T32(); c0, n_c0 = T32(); m0, n_m0 = T32()
            ACT(s0, XI[0], AF.Sin, [XI[1]], [n_s0], scale=1.0 / 16)
            ACT(c0, XI[0], AF.Sin, [XI[1], "cst1"], [n_c0], bias=cst[:, 1:2], scale=1.0 / 16)
            ACT(m0, XR[0], AF.Exp, [XR[1]], [n_m0], scale=1.0 / 16)
            z = (mul((m0, n_m0), (c0, n_c0)), mul((m0, n_m0), (s0, n_s0)))
            for _ in range(4):
                z = cmul(z, z)
            LAM = z
            one, n_one = T32(); zero, n_zero = T32()
            MS("dve", one, 1.0, [], [n_one]); MS("dve", zero, 0.0, [], [n_zero])
            ONE = ((one, n_one), (zero, n_zero))
            pw = [ONE, LAM]
            for e_ in range(2, 9):
                pw.append(cmul(pw[-1], LAM))
            den = add(mul(LAM[0], LAM[0]), mul(LAM[1], LAM[1]))
            rden, n_rden = T32()
            OP("dve", lambda e: e.reciprocal(rden, den[0]), [den[1]], [n_rden])
            RDEN = (rden, n_rden)
            nli, n_nli = T32()
            TS("dve", nli, LAM[1][0], -1.0, None, ALU.mult, None, [LAM[1][1]], [n_nli])
            INV = (mul(LAM[0], RDEN), mul((nli, n_nli), RDEN))
            npw = [ONE, INV]
            for e_ in range(2, 8):
                npw.append(cmul(npw[-1], INV))
            nr, n_nr = T32()
            TS("dve", nr, LAM[0][0], -1.0, None, ALU.add, None, [LAM[0][1]], [n_nr])
            NR = (nr, n_nr)
            d2 = add(mul(AR, AR), mul(AI, AI))
            rd2, n_rd2 = T32()
            OP("dve", lambda e: e.reciprocal(rd2, d2[0]), [d2[1]], [n_rd2])
            RD2 = (rd2, n_rd2)
            W = (mul(add(mul(NR, AR), mul(LAM[1], AI)), RD2), mul(add(mul(LAM[1], AR), mul(NR, AI), ALU.subtract), RD2))
            cP = [cmul(npw[j], W) for j in range(8)]
            cPe = [cmul(pw[7 - j], W) for j in range(8)]
            cQ = [pw[j] for j in range(8)]
            cF = [pw[j + 1] for j in range(8)]

            def coef(cl, s_re_a, s_im_a, s_re_b, s_im_b, tag):
                ta = A.f32(32, 8)
                tb = A.f32(32, 8)
                for j in range(8):
                    (re_, nre), (im_, nim) = cl[j]
                    TS("dve", ta[:, :, j], re_, s_re_a, None, ALU.mult, None, [nre] + CST, [tag + "a"])
                    STT("dve", ta[:, :, j], im_, s_im_a, ta[:, :, j], ALU.mult, ALU.add, [nim, tag + "a"] + CST, [tag + "a"])
                    TS("dve", tb[:, :, j], re_, s_re_b, None, ALU.mult, None, [nre] + CST, [tag + "b"])
                    STT("dve", tb[:, :, j], im_, s_im_b, tb[:, :, j], ALU.mult, ALU.add, [nim, tag + "b"] + CST, [tag + "b"])
                return ta, tb

            CST2 = CST + ["cst5a", "cst5b", "cst6a", "cst6b"]
            CST[:] = CST2
            aP, bP = coef(cP, selLo, selHi, selHi, nselLo, "cfP")
            aE, bE = coef(cPe, selLo, selHi, selHi, nselLo, "cfE")
            aQ, bQ = coef(cQ, selLo, nselHi, nselHi, nselLo, "cfQ")
            aF, bF = coef(cF, selLo, nselHi, nselHi, nselLo, "cfF")
            tmp4 = A.f32(32, 128)

            def stack(ta, tb, Xre, Xim, tag, xr_names, xi_names):
                o = A.f32(32, 128)
                o4 = o.rearrange("p g (j h) -> p g j h", j=8)
                t4 = tmp4.rearrange("p g (j h) -> p g j h", j=8)
                TT("dve", o4, ta.unsqueeze(3).broadcast_to([128, 32, 8, 16]), Xre.unsqueeze(2).broadcast_to([128, 32, 8, 16]),
                   ALU.mult, [tag + "a"] + xr_names, ["st_" + tag])
                TT("dve", t4, tb.unsqueeze(3).broadcast_to([128, 32, 8, 16]), Xim.unsqueeze(2).broadcast_to([128, 32, 8, 16]),
                   ALU.mult, [tag + "b"] + xi_names, ["tmp4"])
                TT("dve", o, o, tmp4, ALU.add, ["st_" + tag, "tmp4"], ["st_" + tag])
                return o

            Pst = stack(aP, bP, Bre, Bim, "cfP", ["Bre0", "Bre1"], ["Bim0", "Bim1"])
            Est = stack(aE, bE, Bre, Bim, "cfE", ["Bre0", "Bre1"], ["Bim0", "Bim1"])
            Qst = stack(aQ, bQ, Cre, Cim, "cfQ", ["Cre"], ["Cim"])
            Fst = stack(aF, bF, Cre, Cim, "cfF", ["Cre"], ["Cim"])
            smat_sb = A.bf16(32, 3, 128)
            tmpT = [A.f32(128), A.f32(128)]
            for g in range(32):
                b = g % 2
                MM(ps[b][:, 0:128], Pst[:, g, :], Qst[:, g, :], True, True, ["st_cfP", "st_cfQ"], ["ps%d" % b])
                TT("dve", tmpT[b], ps[b][:, 0:128], mask, ALU.mult, ["ps%d" % b, "mask"], ["tmpT%d" % b])
                STT("dve", smat_sb[:, g, 0, :], ident_f, dcol[:, g:g + 1], tmpT[b], ALU.mult, ALU.add,
                    ["ident_f", "tmpT%d" % b] + DCOL, ["smat_sb"])
                TR(ps[2 + b][:, 0:128], Est[:, g, :], ident_f, ["st_cfE", "ident_f"], ["ps%d" % (2 + b)])
                CP("act", smat_sb[:, g, 1, :], ps[2 + b][:, 0:128], ["ps%d" % (2 + b)], ["smat_sb"])
            CP("pool", smat_sb[:, :, 2, :], Fst, ["st_cfF"], ["smat_sb"])
            DMA("sp", smat_d, smat_sb.rearrange("p g k c -> p (g k c)"), ["smat_sb"], ["smat_d"])
            CP("dve", AA[:, 0, :], pw[8][0][0], [pw[8][0][1]], ["AA"])
            CP("dve", AA[:, 1, :], pw[8][0][0], [pw[8][0][1]], ["AA"])
            TS("dve", BB[:, 0, :], pw[8][1][0], sgn, None, ALU.mult, None, [pw[8][1][1]] + CST, ["BB"])
            TS("dve", BB[:, 1, :], BB[:, 0, :], -1.0, None, ALU.mult, None, ["BB"], ["BB"])
            A.pop()
            P.barrier()

        if stage >= 2:
            setup_mixer()

        def mixer_phase(src, dst, srcdep):
            A.push()
            Wst = A.f32(2, 32, 128)
            h2T = Wst[:, 1].rearrange("p g n -> p (g n)").bitcast(BF16).rearrange("p (k n) -> p k n", k=8)
            w_io = A.bf16(8, 1280)
            w_o = w_io[:, :, 0:1024]
            wglu = A.bf16(4, 512)
            qT = A.bf16(4, 1024)
            kT = A.bf16(1152)
            vtk = A.bf16(9, 2, 66)
            Uflat = A.bf16(4096)
            Ug = Uflat.rearrange("p (g j h) -> p g j h", g=32, j=8)
            Z = A.bf16(32, 128)
            Xb = A.bf16(32, 128)
            attnT = A.bf16(4, 1024)
            yT = A.bf16(4, 1024)
            y2T = A.bf16(4, 1024)
            smat = A.bf16(32, 3, 128)
            xt = [A.f32(D), A.f32(D)]
            tmp = A.f32(D)
            tmp2 = A.f32(D)
            hb = [A.bf16(D)]
            s_sb = A.f32(4, 256)
            p_bf = A.bf16(4, 256)
            pT = A.bf16(4, 2, 128)
            ao = A.f32(512)
            aob = A.bf16(512)
            sgt = A.f32(512)
            y2p = A.f32(512)
            sqb = A.bf16(512)
            sqb2 = A.bf16(D)
            t1 = A.f32(2, 32)
            t2 = A.f32(2, 32)
            ast = A.f32(64)
            ssa = A.f32(8)
            rsa = A.f32(8)
            rss = A.f32(8)
            Ytok = Uflat.rearrange("p (j c) -> p j c", j=8)
            w_in_v = w_in.rearrange("(k p) n -> p k n", p=128)
            w_out_v = w_out.rearrange("(k p) n -> p k n", p=128)
            DMA("sp", smat.rearrange("p g k c -> p (g k c)"), smat_d, ["smat_d"], ["smat"])
            DMA("pool", wglu, w_glu.rearrange("(k p) n -> p k n", p=128), [], ["wglu"])
            MS("dve", vtk, 1.0, [], ["vtk"])
            rot = [0]

            def bank2():
                rot[0] += 1
                return (rot[0] % 2)

            for u in range(nunits):
                seq, half = u // 2, u % 2
                if half == 0:
                    load_modt(1, seq)
                    MS("pool", Xc, 0.0, [], ["Xc"])
                for tt in range(8):
                    norm_transpose(src, u * 1024 + tt * 128, h2T, tt, xt, tmp, hb, "Wsw", srcdep, None, sqb2)
                DMA("pool", w_io, w_in_v, [], ["w_io"])
                for pr in range(4):
                    for st in range(2):
                        b = bank2()
                        for hh, base in ((pr, 0), (pr + 4, 64)):
                            for k in range(8):
                                MM(ps[b][base:base + 64, :], w_io[:, k, hh * 64:(hh + 1) * 64], h2T[:, k, st * 512:(st + 1) * 512],
                                   k == 0, k == 7, ["w_io", "Wsw"], ["ps%d" % b])
                        CP("act", qT[:, pr, st * 512:(st + 1) * 512], ps[b], ["ps%d" % b], ["qT"])
                for st in range(2):
                    b = bank2()
                    for k in range(8):
                        MM(ps[b], w_io[:, k, 512:640], h2T[:, k, st * 512:(st + 1) * 512], k == 0, k == 7, ["w_io", "Wsw"], ["ps%d" % b])
                    CP("act", kT[:, 128 + st * 512:128 + (st + 1) * 512], ps[b], ["ps%d" % b], ["kT"])
                for tt in range(8):
                    b = bank2()
                    for k in range(8):
                        MM(ps[b][:, 0:128], h2T[:, k, tt * 128:(tt + 1) * 128], w_io[:, k, 640:768], k == 0, k == 7, ["w_io", "Wsw"], ["ps%d" % b])
                    CP("dve", vtk[:, 1 + tt, :, 0:64], ps[b][:, 0:128].rearrange("p (a d) -> p a d", a=2), ["ps%d" % b], ["vtk"])
                for j in range(8):
                    b = bank2()
                    for k in range(8):
                        MM(ps[b], h2T[:, k, :].rearrange("p (n j) -> p n j", j=8)[:, :, j], w_io[:, k, 768:1280], k == 0, k == 7,
                           ["w_io", "Wsw"], ["ps%d" % b])
                    CP("act", Ug[:, :, j, :], ps[b].rearrange("p (g h) -> p g h", g=32), ["ps%d" % b], ["U"])
                for gb in range(4):
                    b = 4 + gb % 2
                    pbz = psb[b].rearrange("p (g n) -> p g n", g=8)
                    for gi in range(8):
                        g = gb * 8 + gi
                        TR(pbz[:, gi, :], Ug[:, g].rearrange("p j h -> p (j h)"), ident_b, ["U", "ident_b"], ["ps%d" % b])
                    CP("dve", Z[:, gb * 8:(gb + 1) * 8, :], pbz, ["ps%d" % b], ["Z"])
                for gb in range(8):
                    b = bank2()
                    for gi in range(4):
                        g = gb * 4 + gi
                        MM(ps[b][:, gi * 128:(gi + 1) * 128], smat[:, g, 1, :], Z[:, g, :], True, True, ["smat", "Z"], ["ps%d" % b])
                    CP("act", Wst[:, 0, gb * 4:(gb + 1) * 4, :].rearrange("p g n -> p (g n)"), ps[b], ["ps%d" % b], ["Wx"])
                    b2_ = 2 + gb % 2
                    MM(ps[b2_], perm_f, Wst[:, 0, gb * 4:(gb + 1) * 4, :].rearrange("p g n -> p (g n)"), True, True,
                       ["perm_a", "perm_b", "Wx"], ["ps%d" % b2_])
                    CP("act", Wst[:, 1, gb * 4:(gb + 1) * 4, :].rearrange("p g n -> p (g n)"), ps[b2_], ["ps%d" % b2_], ["Wsw"])
                CP("pool", Xb[:, :, 0], Xc[:, 0, :], ["Xc"], ["Xb0"])

                def chain(n0, n1):
                    for n in range(n0, n1):
                        if n == 0:
                            prev, prev_sw, prev_x = Xc, Xc[:, 1, :], Xc[:, 0, :]
                        else:
                            prev, prev_sw, prev_x = Wst[:, :, :, n - 1], Wst[:, 1, :, n - 1], Wst[:, 0, :, n - 1]
                        TT("pool", t1, AA, prev, ALU.mult, ["AA", "Wx", "Wsw", "Xc"], ["t1"])
                        TT("pool", t2[:, 0, :], BB[:, 0, :], prev_sw, ALU.mult, ["BB", "Wx", "Wsw", "Xc"], ["t2a"])
                        TT("pool", t2[:, 1, :], BB[:, 1, :], prev_x, ALU.mult, ["BB", "Wx", "Wsw", "Xc"], ["t2b"])
                        TT("pool", t1, t1, t2, ALU.add, ["t1", "t2a", "t2b"], ["t1"])
                        TT("pool", Wst[:, :, :, n], Wst[:, :, :, n], t1, ALU.add, ["t1", "Wx", "Wsw"], ["Wx", "Wsw"])

                chain(0, 64)
                CP("pool", Xb[:, :, 1:65], Wst[:, 0, :, 0:64], ["Wx"], ["Xb1h0"])
                DMA("pool", w_o, w_out_v, [], ["w_io"])
                for bi in range(8):
                    first = (half == 0 and bi == 0)
                    for kv in range(2):
                        sb_ = (2 * bi + kv) % 2
                        psS = [ps[2 * sb_], ps[2 * sb_ + 1]]
                        pr_ = slice(kv * 64, (kv + 1) * 64)
                        c0_ = 128 if first else 0
                        for hh in range(4):
                            o_ = psS[hh // 2][:, (hh % 2) * 256 + c0_:(hh % 2) * 256 + 256]
                            MM(o_, qT[pr_, hh, bi * 128:(bi + 1) * 128], kT[pr_, bi * 128 + c0_:bi * 128 + 256], True, True,
                               ["qT", "kT"], ["ps%d" % (2 * sb_ + hh // 2)])
                        for hp in range(2):
                            STT("dve", s_sb[:, 2 * hp:2 * hp + 2, c0_:256], psS[hp].rearrange("p (a b) -> p a b", a=2)[:, :, c0_:256], 0.125,
                                atab[:, kv * 4 + 2 * hp:kv * 4 + 2 * hp + 2, c0_:256], ALU.mult, ALU.add,
                                ["ps%d" % (2 * sb_ + hp), "atab"], ["s_sb%d" % hp])
                        OP("dve", lambda e, c0_=c0_: e.reduce_max(ast[:, 0:4], s_sb[:, :, c0_:256], AX.X), ["s_sb0", "s_sb1"], ["ast_m"])
                        TT("dve", ast[:, 0:4], ast[:, 0:4], sink_t[:, kv * 4:kv * 4 + 4], ALU.max, ["ast_m", "sink_t"], ["ast_m"])
                        TS("dve", ast[:, 4:8], ast[:, 0:4], -1.0, None, ALU.mult, None, ["ast_m"], ["ast_nm"])
                        TT("dve", ast[:, 8:12], sink_t[:, kv * 4:kv * 4 + 4], ast[:, 0:4], ALU.subtract, ["ast_m", "sink_t"], ["ast_d"])
                        ACT(ast[:, 12:16], ast[:, 8:12], AF.Exp, ["ast_d"], ["ast_es"])
                        for hh in range(4):
                            ACT(p_bf[:, hh, c0_:256], s_sb[:, hh, c0_:256], AF.Exp, ["s_sb0", "s_sb1", "ast_nm"], ["p_bf"], bias=ast[:, 4 + hh:5 + hh])
                        tb_ = 4 + sb_
                        pbt = psb[tb_].rearrange("p (h k q) -> p h k q", h=4, k=2)
                        for hh in range(4):
                            for kh in range(1 if first else 0, 2):
                                TR(pbt[:, hh, kh, :], p_bf[:, hh, kh * 128:(kh + 1) * 128], ident_b, ["p_bf", "ident_b"], ["ps%d" % tb_])
                        if first:
                            CP("act", pT[:, :, 1, :], pbt[:, :, 1, :], ["ps%d" % tb_], ["pT"])
                        else:
                            CP("act", pT, pbt, ["ps%d" % tb_], ["pT"])
                        ob_ = 6 + sb_
                        psO = ps[ob_][:, 0:260].rearrange("p (h d) -> p h d", h=4)
                        for hh in range(4):
                            khs = [1] if first else [0, 1]
                            for kh in khs:
                                MM(psO[:, hh, :], pT[:, hh, kh, :], vtk[:, bi + kh, kv, 0:65], kh == khs[0], kh == 1,
                                   ["pT", "vtk"], ["ps%d" % ob_])
                        TT("dve", ast[:, 16:20], psO[:, :, 64], ast[:, 12:16], ALU.add, ["ps%d" % ob_, "ast_es"], ["ast_den"])
                        OP("dve", lambda e: e.reciprocal(ast[:, 20:24], ast[:, 16:20]), ["ast_den"], ["ast_rd"])
                        TT("dve", ao[:, kv * 256:(kv + 1) * 256].rearrange("p (h d) -> p h d", h=4), psO[:, :, 0:64],
                           ast[:, 20:24].unsqueeze(2).broadcast_to([128, 4, 64]), ALU.mult, ["ps%d" % ob_, "ast_rd"], ["ao%d" % kv])
                    MS("dve", ssa[:, bi:bi + 1], 0.0, [], ["ssa%d" % bi])
                    ACT(tmp2[:, 0:512], ao, AF.Square, ["ao0", "ao1", "ssa%d" % bi], ["tmp2", "ssa%d" % bi], accum=ssa[:, bi:bi + 1])
                    TT("dve", aob, ao, gattn_t, ALU.mult, ["ao0", "ao1", "gattn_t"], ["aob"])
                    tb_ = 4 + bi % 2
                    pba = psb[tb_][:, 0:512].rearrange("p (c q) -> p c q", c=4)
                    for cc in range(4):
                        TR(pba[:, cc, :], aob[:, cc * 128:(cc + 1) * 128], ident_b, ["aob", "ident_b"], ["ps%d" % tb_])
                    CP("act", attnT[:, :, bi * 128:(bi + 1) * 128], pba, ["ps%d" % tb_], ["attnT"])
                chain(64, 128)
                CP("pool", Xc, Wst[:, :, :, 127], ["Wx", "Wsw"], ["Xc"])
                CP("pool", Xb[:, :, 65:128], Wst[:, 0, :, 64:127], ["Wx"], ["Xb1h1"])
                CP("dve", kT[:, 0:128], kT[:, 1024:1152], ["kT"], ["kT"])
                CP("dve", vtk[:, 0, :, 0:64], vtk[:, 8, :, 0:64], ["vtk"], ["vtk"])
                ACT(rsa, ssa, AF.Sqrt, ["ssa%d" % i_ for i_ in range(8)] + ["cst0"], ["rsa"], bias=eps_c, scale=1.0 / 512)
                OP("dve", lambda e: e.reciprocal(rsa, rsa), ["rsa"], ["rsa"])
                for hf_ in range(2):
                    pp = slice(hf_ * 64, (hf_ + 1) * 64)
                    hn = "_h%d" % hf_
                    for gb in range(8):
                        b = bank2()
                        for gi in range(4):
                            g = gb * 4 + gi
                            o_ = ps[b][pp, gi * 128:(gi + 1) * 128]
                            MM(o_, Z[:, g, hf_ * 64:(hf_ + 1) * 64], smat[:, g, 0, :], True, False, ["Z", "smat"], ["ps%d" % b])
                            MM(o_, Xb[:, g, hf_ * 64:(hf_ + 1) * 64], smat[:, g, 2, :], False, True, ["Xb0", "Xb1" + ("h%d" % hf_), "smat"], ["ps%d" % b])
                        ACT(Ytok[pp, :, gb * 64:(gb + 1) * 64].rearrange("p j (g h) -> p g j h", g=4),
                            ps[b][pp, :].rearrange("p (g j h) -> p g j h", g=4, j=8), AF.Gelu_apprx_tanh, ["ps%d" % b], ["U" + hn, "U"])
                    for cc in range(4):
                        b = 4 + cc % 2
                        pby = psb[b][:, 0:512].rearrange("p (j n) -> p j n", j=8)
                        for j in range(8):
                            TR(pby[:, j, :], Ytok[pp, j, cc * 128:(cc + 1) * 128], ident_b[pp, pp], ["U" + hn, "ident_b"], ["ps%d" % b])
                        CP("dve", yT[:, cc, hf_ * 512:(hf_ + 1) * 512].rearrange("p (n j) -> p j n", j=8), pby, ["ps%d" % b], ["yT" + hn])
                    st = hf_
                    pcol = 300 + 40 * hf_
                    pstat32 = ps[7][:, pcol:pcol + 16].rearrange("p (o t) -> p o t", o=4)
                    for oc in range(4):
                        b = bank2()
                        for kc in range(4):
                            MM(ps[b], wglu[:, kc, oc * 128:(oc + 1) * 128], yT[:, kc, st * 512:(st + 1) * 512], kc == 0, kc == 3,
                               ["wglu", "yT" + hn], ["ps%d" % b])
                        ACT(sgt, ps[b], AF.Sigmoid, ["ps%d" % b, "bglu_c"], ["sgt"], bias=bglu_c[:, oc:oc + 1])
                        TT("dve", y2p, yT[:, oc, st * 512:(st + 1) * 512], sgt, ALU.mult, ["yT" + hn, "sgt"], ["y2p"])
                        ACT(sqb, y2p, AF.Square, ["y2p"], ["sqb"])
                        TS("dve", y2T[:, oc, st * 512:(st + 1) * 512], y2p, gssm_c[:, oc:oc + 1], None, ALU.mult, None, ["y2p", "gssm_c"], ["y2T" + hn])
                        for t4 in range(4):
                            MM(pstat32[:, oc, t4:t4 + 1], sqb[:, t4 * 128:(t4 + 1) * 128], ones_b[:, 0:1], True, True,
                               ["sqb", "ones_b"], ["pstat" + hn, "ps7"])
                    rssh = rss[:, hf_ * 4:(hf_ + 1) * 4]
                    OP("dve", lambda e, pcol=pcol, rssh=rssh: e.reduce_sum(rssh, ps[7][:, pcol:pcol + 16].rearrange("p (o t) -> p t o", o=4), AX.X),
                       ["pstat" + hn, "ps7"], ["rss" + hn])
                    ACT(rssh, rssh, AF.Sqrt, ["rss" + hn, "cst0"], ["rss" + hn], bias=eps_c, scale=1.0 / 512)
                    OP("dve", lambda e, rssh=rssh: e.reciprocal(rssh, rssh), ["rss" + hn], ["rss" + hn])
                    for tt in range(hf_ * 4, hf_ * 4 + 4):
                        pa = [0, 1] if tt % 2 == 0 else [2, 3]
                        for hf in range(2):
                            for kc in range(4):
                                MM(ps[pa[hf]], attnT[:, kc, tt * 128:(tt + 1) * 128], w_o[:, kc, hf * 512:(hf + 1) * 512], kc == 0, kc == 3,
                                   ["attnT", "w_io"], ["ps%d" % pa[hf]])
                        for hf in range(2):
                            for kc in range(4):
                                MM(ps[4 + hf], y2T[:, kc, tt * 128:(tt + 1) * 128], w_o[:, 4 + kc, hf * 512:(hf + 1) * 512], kc == 0, kc == 3,
                                   ["y2T" + hn, "w_io"], ["ps%d" % (4 + hf)])
                        for hf in range(2):
                            sl = slice(hf * 512, (hf + 1) * 512)
                            TS("dve", tmp2[:, sl], ps[pa[hf]], rsa[:, tt:tt + 1], None, ALU.mult, None, ["ps%d" % pa[hf], "rsa"], ["tmp2"])
                            STT("dve", tmp2[:, sl], ps[4 + hf], rss[:, tt:tt + 1], tmp2[:, sl], ALU.mult, ALU.add,
                                ["ps%d" % (4 + hf), "rss" + hn, "tmp2"], ["tmp2"])
                        b = tt % 2
                        DMA("sp", xt[b], src[u * 1024 + tt * 128:u * 1024 + (tt + 1) * 128, :], list(srcdep), ["xt%d" % b])
                        MS("dve", stat[:, 4:5], 0.0, [], ["st4"])
                        ACT(sqb2, tmp2, AF.Square, ["tmp2", "st4"], ["st4"], accum=stat[:, 4:5])
                        ACT(stat[:, 7:8], stat[:, 4:5], AF.Sqrt, ["st4", "cst0"], ["st7"], bias=eps_c, scale=1.0 / D)
                        OP("dve", lambda e: e.reciprocal(stat[:, 8:9], stat[:, 7:8]), ["st7"], ["st8"])
                        STT("dve", tmp, tmp2, stat[:, 8:9], modt[:, 2, :], ALU.mult, ALU.mult, ["tmp2", "st8", "modt", "tmp"], ["tmp"])
                        TT("pool", xt[b], tmp, xt[b], ALU.add, ["tmp", "xt%d" % b], ["xt%d" % b])
                        DMA("sp", dst[u * 1024 + tt * 128:u * 1024 + (tt + 1) * 128, :], xt[b], ["xt%d" % b], ["dst"])
            A.pop()
            P.barrier()

        ffn_phase(0, x, x1d if stage >= 2 else out)
        if stage >= 2:
            mixer_phase(x1d, x2d if stage >= 3 else out, ["dst"])
        if stage >= 3:
            ffn_phase(2, x2d, out, ["dst"])

        OP("sp", None, ["dst", "tabs"], [])
        P.emit()
        nc._prog_stats = P.stats
    return nc


_INPUT_ORDER = None


def kernel(**inputs):
    stage = int(inputs.pop("_stage", 3)) if "_stage" in inputs else 3
    f = lambda a: np.ascontiguousarray(np.asarray(a, dtype=np.float32))
    x = f(inputs["x"])
    c = f(inputs["c"])
    shared = {}
    for n in ("w_ada", "b_ada", "g_pre_ff1", "g_post_ff1", "w1_gate", "w1_up", "w1_down", "g_pre_mix", "g_post_mix",
              "w_in", "attn_sinks", "ssm_a_re", "ssm_a_im", "ssm_log_step", "ssm_b_re", "ssm_b_im", "ssm_c_re",
              "ssm_c_im", "ssm_d", "ssm_w_glu", "ssm_b_glu", "g_attn_out", "g_ssm_out", "w_out", "g_pre_ff2",
              "g_post_ff2", "w2_gate", "w2_up", "w2_down"):
        a = f(inputs[n])[0]
        if n in ("ssm_c_re", "ssm_c_im"):
            a = np.ascontiguousarray(a.reshape(512, 64))
        shared[n] = a
    nc = build(stage)
    in_maps = []
    for i in range(8):
        m = dict(shared)
        m["x"] = np.ascontiguousarray(x[2 * i:2 * i + 2].reshape(TOK, D))
        m["c"] = np.ascontiguousarray(c[2 * i:2 * i + 2])
        in_maps.append(m)
    res = run_bass_kernel_spmd(nc, in_maps, core_ids=list(range(8)))
    outs = [np.asarray(r["out"]).reshape(2, 2048, D) for r in res.results]
    return np.concatenate(outs, axis=0).astype(np.float32)
```
